# Optimizing a Trainium2 kernel written in Bass

```python
import jax, jax.numpy as jnp
from jax import lax
import numpy as np

D_MODEL = 1024
BATCH = 2
SEQ = 8192
DEPTH = 2

NSA_HEADS = 8
NSA_GROUPS = 2
NSA_HPG = NSA_HEADS // NSA_GROUPS
NSA_DK = 64
NSA_DV = 64
CMP_BLOCK = 32
CMP_STRIDE = 16
SLC_BLOCK = 64
N_SELECT = 16
WINDOW = 512
MLA_HEADS = 8
MLA_NOPE = 64
MLA_ROPE = 32
MLA_V = 64
MLA_Q_RANK = 384
MLA_KV_RANK = 256
ROPE_THETA = 10000.0
D_FF = 2816
N_MOD = 9
Q_BLOCK = 128
NORM_EPS = 1e-6

IN_SPLITS = (
    NSA_HEADS * NSA_DK,
    NSA_GROUPS * NSA_DK,
    NSA_GROUPS * NSA_DV,
    NSA_GROUPS * NSA_DK,
    NSA_GROUPS * NSA_DV,
    NSA_GROUPS * NSA_DK,
    NSA_GROUPS * NSA_DV,
    3 * NSA_HEADS,
    MLA_Q_RANK,
    MLA_KV_RANK,
    MLA_ROPE,
    2 * D_MODEL,
)
IN_TOTAL = sum(IN_SPLITS)

kernel_name = "hybrid_nsa_mla_macaron_adaln"


def rms_norm(x):
    xf = x.astype(jnp.float32)
    return (xf * lax.rsqrt(jnp.mean(xf * xf, -1, keepdims=True) + NORM_EPS)).astype(x.dtype)


def modulate(x, shift, scale):
    return rms_norm(x) * (1 + scale[:, None, :]) + shift[:, None, :]


def swiglu(u, w_gate, w_up, w_down):
    return (jax.nn.silu(u @ w_gate) * (u @ w_up)) @ w_down


def masked_softmax(scores, mask):
    s = jnp.where(mask, scores.astype(jnp.float32), -jnp.inf)
    m = jnp.max(s, -1, keepdims=True)
    m = jnp.where(jnp.isfinite(m), m, 0.0)
    p = jnp.exp(s - m)
    return p / jnp.maximum(jnp.sum(p, -1, keepdims=True), 1.0)


def rope(x, cos, sin):
    half = x.shape[-1] // 2
    x1, x2 = x[..., :half], x[..., half:]
    return jnp.concatenate([x1 * cos - x2 * sin, x1 * sin + x2 * cos], -1).astype(x.dtype)


def alibi_slopes(n):
    return 2.0 ** (-8.0 * jnp.arange(1, n + 1, dtype=jnp.float32) / n)


def compress(k, w1, w2, pe):
    B, G, S, d = k.shape
    n_chunks = S // CMP_STRIDE
    r = CMP_BLOCK // CMP_STRIDE
    n_cmp = n_chunks - r + 1
    chunks = k.reshape(B, G, n_chunks, CMP_STRIDE, d)
    blocks = jnp.concatenate([chunks[:, :, j:j + n_cmp] for j in range(r)], axis=3)
    blocks = (blocks + pe).reshape(B, G, n_cmp, CMP_BLOCK * d)
    return jax.nn.gelu(blocks @ w1) @ w2


def nsa_attention(q, k_c, v_c, k_s, v_s, k_w, v_w, gates):
    B, G, HPG, S, dk = q.shape
    dv = v_c.shape[-1]
    nq = S // Q_BLOCK
    n_slc = S // SLC_BLOCK
    n_top = min(N_SELECT, n_slc)
    n_cmp = k_c.shape[2]
    ratio = SLC_BLOCK // CMP_STRIDE
    r = CMP_BLOCK // CMP_STRIDE
    cmp_end = jnp.arange(n_cmp) * CMP_STRIDE + CMP_BLOCK - 1
    ks_blocks = k_s.reshape(B, G, n_slc, SLC_BLOCK, dk)
    vs_blocks = v_s.reshape(B, G, n_slc, SLC_BLOCK, dv)
    kw_pad = jnp.pad(k_w, ((0, 0), (0, 0), (WINDOW, 0), (0, 0)))
    vw_pad = jnp.pad(v_w, ((0, 0), (0, 0), (WINDOW, 0), (0, 0)))
    slope = alibi_slopes(NSA_HEADS).reshape(1, G, HPG, 1, 1)
    scale = dk ** -0.5
    b_ix = jnp.arange(B)[:, None, None, None]
    g_ix = jnp.arange(G)[None, :, None, None]
    blk = jnp.arange(n_slc)

    def block(i):
        q0 = i * Q_BLOCK
        qb = lax.dynamic_slice_in_dim(q, q0, Q_BLOCK, axis=3)
        gb = lax.dynamic_slice_in_dim(gates, q0, Q_BLOCK, axis=3)
        t = q0 + jnp.arange(Q_BLOCK)
        d_c = (t[:, None] - cmp_end[None, :]).astype(jnp.float32)
        s_c = jnp.einsum('bghqd,bgnd->bghqn', qb, k_c).astype(jnp.float32) * scale - slope * d_c
        p_c = masked_softmax(s_c, cmp_end[None, :] <= t[:, None])
        o_c = jnp.einsum('bghqn,bgnd->bghqd', p_c.astype(v_c.dtype), v_c)
        imp = jnp.pad(p_c.sum(axis=2), ((0, 0), (0, 0), (0, 0), (r - 1, n_slc * ratio - n_cmp)))
        imp_s = imp[..., 0:ratio * n_slc:ratio]
        for j in range(1, ratio + r - 1):
            imp_s = imp_s + imp[..., j:j + ratio * n_slc:ratio]
        cur = t // SLC_BLOCK
        forced = (blk[None, :] == 0) | (blk[None, :] == cur[:, None]) | (blk[None, :] == cur[:, None] - 1)
        valid = blk[None, :] * SLC_BLOCK <= t[:, None]
        imp_s = jnp.where(forced, jnp.inf, jnp.where(valid, imp_s, -jnp.inf))
        _, idx = lax.top_k(imp_s, n_top)
        ks = ks_blocks[b_ix, g_ix, idx].reshape(B, G, Q_BLOCK, n_top * SLC_BLOCK, dk)
        vs = vs_blocks[b_ix, g_ix, idx].reshape(B, G, Q_BLOCK, n_top * SLC_BLOCK, dv)
        pos_s = (idx[..., None] * SLC_BLOCK + jnp.arange(SLC_BLOCK)).reshape(B, G, Q_BLOCK, n_top * SLC_BLOCK)
        d_s = (t[:, None] - pos_s).astype(jnp.float32)[:, :, None]
        s_s = jnp.einsum('bghqd,bgqkd->bghqk', qb, ks).astype(jnp.float32) * scale - slope * d_s
        p_s = masked_softmax(s_s, (pos_s <= t[:, None])[:, :, None])
        o_s = jnp.einsum('bghqk,bgqkd->bghqd', p_s.astype(vs.dtype), vs)
        kw = lax.dynamic_slice_in_dim(kw_pad, q0, WINDOW + Q_BLOCK, axis=2)
        vw = lax.dynamic_slice_in_dim(vw_pad, q0, WINDOW + Q_BLOCK, axis=2)
        pos_w = q0 - WINDOW + jnp.arange(WINDOW + Q_BLOCK)
        d_w = t[:, None] - pos_w[None, :]
        mask_w = (d_w >= 0) & (d_w < WINDOW) & (pos_w[None, :] >= 0)
        s_w = jnp.einsum('bghqd,bgkd->bghqk', qb, kw).astype(jnp.float32) * scale - slope * d_w.astype(jnp.float32)
        p_w = masked_softmax(s_w, mask_w)
        o_w = jnp.einsum('bghqk,bgkd->bghqd', p_w.astype(vw.dtype), vw)
        return gb[..., 0:1] * o_c + gb[..., 1:2] * o_s + gb[..., 2:3] * o_w

    out = lax.map(block, jnp.arange(nq))
    return out.transpose(1, 0, 4, 2, 3, 5).reshape(B, S, G * HPG * dv)


def mla_attention(q_nope, q_rope, k_nope, k_rope, v):
    B, H, S, _ = q_nope.shape
    dv = v.shape[-1]
    nq = S // Q_BLOCK
    scale = (MLA_NOPE + MLA_ROPE) ** -0.5
    kpos = jnp.arange(S)

    def block(i):
        q0 = i * Q_BLOCK
        qn = lax.dynamic_slice_in_dim(q_nope, q0, Q_BLOCK, axis=2)
        qr = lax.dynamic_slice_in_dim(q_rope, q0, Q_BLOCK, axis=2)
        s = (jnp.einsum('bhqd,bhkd->bhqk', qn, k_nope)
             + jnp.einsum('bhqd,bkd->bhqk', qr, k_rope)).astype(jnp.float32) * scale
        qpos = q0 + jnp.arange(Q_BLOCK)
        p = masked_softmax(s, kpos[None, :] <= qpos[:, None])
        return jnp.einsum('bhqk,bhkd->bhqd', p.astype(v.dtype), v)

    out = lax.map(block, jnp.arange(nq))
    return out.transpose(1, 0, 3, 2, 4).reshape(B, S, H * dv)


def token_mix(u, cos, sin, w_in, cmpk_w1, cmpk_w2, cmpk_pe, cmpv_w1, cmpv_w2, cmpv_pe,
              mla_q_norm, mla_w_uq, mla_kv_norm, mla_w_uk, mla_w_uv,
              w_branch_a, w_branch_b, w_out):
    B, S, _ = u.shape
    points = np.cumsum(IN_SPLITS)[:-1].tolist()
    (q_a, kc, vc, ks, vs, kw, vw, g_nsa, cq, ckv, kr, g_merge) = jnp.split(u @ w_in, points, axis=-1)
    q_a = q_a.reshape(B, S, NSA_GROUPS, NSA_HPG, NSA_DK).transpose(0, 2, 3, 1, 4)
    grp = lambda t: t.reshape(B, S, NSA_GROUPS, -1).transpose(0, 2, 1, 3)
    g_nsa = jax.nn.sigmoid(g_nsa).reshape(B, S, NSA_GROUPS, NSA_HPG, 3).transpose(0, 2, 3, 1, 4)
    k_c = compress(grp(kc), cmpk_w1, cmpk_w2, cmpk_pe)
    v_c = compress(grp(vc), cmpv_w1, cmpv_w2, cmpv_pe)
    o_a = nsa_attention(q_a, k_c, v_c, grp(ks), grp(vs), grp(kw), grp(vw), g_nsa)
    q = (rms_norm(cq) * mla_q_norm) @ mla_w_uq
    q = q.reshape(B, S, MLA_HEADS, MLA_NOPE + MLA_ROPE)
    q_nope = q[..., :MLA_NOPE].transpose(0, 2, 1, 3)
    q_rope = rope(q[..., MLA_NOPE:], cos[:, None, :], sin[:, None, :]).transpose(0, 2, 1, 3)
    ckv = rms_norm(ckv) * mla_kv_norm
    k_nope = (ckv @ mla_w_uk).reshape(B, S, MLA_HEADS, MLA_NOPE).transpose(0, 2, 1, 3)
    v = (ckv @ mla_w_uv).reshape(B, S, MLA_HEADS, MLA_V).transpose(0, 2, 1, 3)
    k_rope = rope(kr, cos, sin)
    o_b = mla_attention(q_nope, q_rope, k_nope, k_rope, v)
    g_a, g_b = jnp.split(g_merge, 2, axis=-1)
    y = jax.nn.sigmoid(g_a) * (o_a @ w_branch_a) + jax.nn.sigmoid(g_b) * (o_b @ w_branch_b)
    return y @ w_out


def setup_inputs(seed: int = 0) -> dict:
    key = jax.random.key(seed)
    keys = iter(jax.random.split(key, 40))
    nrm = lambda shape, s: jax.random.normal(next(keys), shape, jnp.float32) * s
    L, D = DEPTH, D_MODEL
    return {
        "x": nrm((BATCH, SEQ, D), 1.0),
        "c": nrm((BATCH, D), 1.0),
        "w_ada": nrm((L, D, N_MOD * D), D ** -0.5),
        "b_ada": nrm((L, N_MOD * D), 0.02),
        "ffn1_gate": nrm((L, D, D_FF), D ** -0.5),
        "ffn1_up": nrm((L, D, D_FF), D ** -0.5),
        "ffn1_down": nrm((L, D_FF, D), D_FF ** -0.5),
        "ffn2_gate": nrm((L, D, D_FF), D ** -0.5),
        "ffn2_up": nrm((L, D, D_FF), D ** -0.5),
        "ffn2_down": nrm((L, D_FF, D), D_FF ** -0.5),
        "w_in": nrm((L, D, IN_TOTAL), D ** -0.5),
        "cmpk_w1": nrm((L, CMP_BLOCK * NSA_DK, NSA_DK), (CMP_BLOCK * NSA_DK) ** -0.5),
        "cmpk_w2": nrm((L, NSA_DK, NSA_DK), NSA_DK ** -0.5),
        "cmpk_pe": nrm((L, CMP_BLOCK, NSA_DK), 0.5),
        "cmpv_w1": nrm((L, CMP_BLOCK * NSA_DV, NSA_DV), (CMP_BLOCK * NSA_DV) ** -0.5),
        "cmpv_w2": nrm((L, NSA_DV, NSA_DV), NSA_DV ** -0.5),
        "cmpv_pe": nrm((L, CMP_BLOCK, NSA_DV), 0.5),
        "mla_q_norm": 1.0 + nrm((L, MLA_Q_RANK), 0.02),
        "mla_w_uq": nrm((L, MLA_Q_RANK, MLA_HEADS * (MLA_NOPE + MLA_ROPE)), MLA_Q_RANK ** -0.5),
        "mla_kv_norm": 1.0 + nrm((L, MLA_KV_RANK), 0.02),
        "mla_w_uk": nrm((L, MLA_KV_RANK, MLA_HEADS * MLA_NOPE), MLA_KV_RANK ** -0.5),
        "mla_w_uv": nrm((L, MLA_KV_RANK, MLA_HEADS * MLA_V), MLA_KV_RANK ** -0.5),
        "w_branch_a": nrm((L, NSA_HEADS * NSA_DV, D), (NSA_HEADS * NSA_DV) ** -0.5),
        "w_branch_b": nrm((L, MLA_HEADS * MLA_V, D), (MLA_HEADS * MLA_V) ** -0.5),
        "w_out": nrm((L, D, D), D ** -0.5),
        "final_norm": 1.0 + nrm((D,), 0.02),
    }


def reference(x, c, w_ada, b_ada, ffn1_gate, ffn1_up, ffn1_down, ffn2_gate, ffn2_up, ffn2_down,
              w_in, cmpk_w1, cmpk_w2, cmpk_pe, cmpv_w1, cmpv_w2, cmpv_pe,
              mla_q_norm, mla_w_uq, mla_kv_norm, mla_w_uk, mla_w_uv,
              w_branch_a, w_branch_b, w_out, final_norm):
    S = x.shape[1]
    half = MLA_ROPE // 2
    inv_freq = ROPE_THETA ** (-jnp.arange(half, dtype=jnp.float32) / half)
    ang = jnp.arange(S, dtype=jnp.float32)[:, None] * inv_freq[None, :]
    cos, sin = jnp.cos(ang).astype(x.dtype), jnp.sin(ang).astype(x.dtype)
    c_act = jax.nn.silu(c)
    for l in range(DEPTH):
        mod = c_act @ w_ada[l] + b_ada[l]
        sh1, sc1, g1, sh2, sc2, g2, sh3, sc3, g3 = jnp.split(mod, N_MOD, axis=-1)
        x = x + 0.5 * g1[:, None, :] * swiglu(modulate(x, sh1, sc1), ffn1_gate[l], ffn1_up[l], ffn1_down[l])
        x = x + g2[:, None, :] * token_mix(
            modulate(x, sh2, sc2), cos, sin, w_in[l],
            cmpk_w1[l], cmpk_w2[l], cmpk_pe[l], cmpv_w1[l], cmpv_w2[l], cmpv_pe[l],
            mla_q_norm[l], mla_w_uq[l], mla_kv_norm[l], mla_w_uk[l], mla_w_uv[l],
            w_branch_a[l], w_branch_b[l], w_out[l])
        x = x + 0.5 * g3[:, None, :] * swiglu(modulate(x, sh3, sc3), ffn2_gate[l], ffn2_up[l], ffn2_down[l])
    return rms_norm(x) * final_norm
```

```python
import contextlib
import numpy as np
import concourse.bass as bass
import concourse.mybir as mybir
from concourse.bass_utils import run_bass_kernel_spmd

F32 = mybir.dt.float32
BF16 = mybir.dt.bfloat16
AF = mybir.ActivationFunctionType
ALU = mybir.AluOpType
AX = mybir.AxisListType

N_DMA_SEMS = 8
D = 1024
DFF = 2816
NT = 2048
NTILE = 16
EPS = 1e-6


class Ctx:
    def __init__(self, nc, stack):
        self.nc = nc
        self.stack = stack
        self.eng = {"pe": nc.tensor, "act": nc.scalar, "dve": nc.vector,
                    "pool": nc.gpsimd, "sp": nc.sync}
        self.sem = {}
        self.cnt = {}
        for e in ["pe", "act", "dve", "pool"]:
            self.sem[e] = stack.enter_context(nc.semaphore("s_" + e))
            self.cnt[e] = 0
        self.dsem = {}
        self.dcnt = {}
        self.drr = {}
        for q in ["sp", "pool", "act"]:
            self.dsem[q] = [stack.enter_context(nc.semaphore(f"d_{q}{i}")) for i in range(N_DMA_SEMS)]
            self.dcnt[q] = [0] * N_DMA_SEMS
            self.drr[q] = 0
        self.seen = {e: {} for e in self.eng}
        self.lastw = {}
        self.reads = {}
        self.semobj = {}
        for e in self.sem:
            self.semobj[("c", e)] = self.sem[e]
        for q in self.dsem:
            for i, s in enumerate(self.dsem[q]):
                self.semobj[("d", q, i)] = s
        self.uid = 0

    def sb(self, name, shape, dt, stack=None):
        self.uid += 1
        return (stack or self.stack).enter_context(self.nc.sbuf_tensor(f"{name}_{self.uid}", list(shape), dt))

    def ps(self, name, shape, dt=F32, stack=None):
        self.uid += 1
        return (stack or self.stack).enter_context(self.nc.psum_tensor(f"{name}_{self.uid}", list(shape), dt))

    def _deps(self, r, w):
        deps = {}

        def add(d):
            if d is None:
                return
            k, v = d
            if deps.get(k, 0) < v:
                deps[k] = v
        for k in r:
            add(self.lastw.get(k))
        for k in w:
            add(self.lastw.get(k))
            for d in self.reads.get(k, []):
                add(d)
        return deps

    def _wait(self, e, deps):
        eng = self.eng[e]
        for k, v in deps.items():
            if self.seen[e].get(k, 0) >= v:
                continue
            if k == ("c", e) and (e == "pe" or v > self.cnt[e]):
                continue
            eng.wait_ge(self.semobj[k], v)
            self.seen[e][k] = v

    def _record(self, tok, r, w):
        for k in r:
            lst = self.reads.setdefault(k, [])
            for i, (kk, vv) in enumerate(lst):
                if kk == tok[0]:
                    if vv < tok[1]:
                        lst[i] = tok
                    break
            else:
                lst.append(tok)
        for k in w:
            self.lastw[k] = tok
            self.reads[k] = []

    def op(self, e, fn, r=(), w=(), inc=True):
        deps = self._deps(r, w)
        self._wait(e, deps)
        ins = fn()
        if inc:
            self.cnt[e] += 1
            ins.then_inc(self.sem[e], 1)
            tok = (("c", e), self.cnt[e])
        else:
            tok = (("c", e), self.cnt[e] + 1)
        self._record(tok, r, w)
        return ins

    def dma(self, q, out, in_, r=(), w=(), **kw):
        deps = self._deps(r, w)
        i = self.drr[q]
        self.drr[q] = (i + 1) % N_DMA_SEMS
        key = ("d", q, i)
        if self.dcnt[q][i] > 0:
            deps[key] = max(deps.get(key, 0), self.dcnt[q][i])
        self._wait(q, deps)
        ins = self.eng[q].dma_start(out=out, in_=in_, **kw)
        self.dcnt[q][i] += 16
        ins.then_inc(self.dsem[q][i], 16)
        tok = (key, self.dcnt[q][i])
        self._record(tok, r, w)
        return ins

    def barrier(self):
        deps = {}
        for e in self.cnt:
            if self.cnt[e] > 0:
                deps[("c", e)] = self.cnt[e]
        for q in self.dcnt:
            for i, v in enumerate(self.dcnt[q]):
                if v > 0:
                    deps[("d", q, i)] = v
        for e in self.eng:
            self._wait(e, dict(deps))
        self.lastw = {}
        self.reads = {}

    @contextlib.contextmanager
    def scope(self):
        with contextlib.ExitStack() as st:
            yield st
            self.barrier()


def emit_consts(c, ident_d):
    nc = c.nc
    idb = c.sb("idb", [128, 128], BF16)
    idf = c.sb("idf", [128, 128], F32)
    c.dma("pool", idb[:], ident_d, w=["idb"])
    c.dma("sp", idf[:], ident_d, w=["idf"])
    return idb, idf


def emit_mod(c, cT_d, wada_d, bada_d, mod_d, st):
    nc = c.nc
    cT = c.sb("cT", [128, 8], F32, st)
    cact = c.sb("cact", [128, 8], F32, st)
    c.dma("sp", cT[:], cT_d, w=["cT"])
    c.op("act", lambda: nc.scalar.activation(out=cact[:], in_=cT[:], func=AF.Silu), r=["cT"], w=["cact"])
    cbc = c.sb("cbc", [128, 8, 128], F32, st)
    c.op("dve", lambda: nc.vector.tensor_copy(out=cbc[:], in_=cact[:].unsqueeze(2).to_broadcast([128, 8, 128])), r=["cact"], w=["cbc"])
    brow = c.sb("brow", [1, 9216], F32, st)
    c.dma("sp", brow[:], bada_d, w=["brow"])
    mrow = c.sb("mrow", [1, 9216], F32, st)
    wv = wada_d.rearrange("(k p) n -> p k n", p=128)
    wt = [c.sb(f"wada{i}", [128, 8, 512], F32, st) for i in range(2)]
    pm = [c.ps(f"pmod{i}", [128, 512], F32, st) for i in range(2)]
    for ch in range(18):
        b = ch % 2
        c.dma("sp", wt[b][:], wv[:, :, ch * 512:(ch + 1) * 512], w=[f"wada{b}"])
        for k in range(8):
            c.op("pe", lambda k=k: nc.tensor.matmul(pm[b][:], lhsT=cbc[:, k, :], rhs=wt[b][:, k, :],
                                                    start=(k == 0), stop=(k == 7)),
                 r=["cbc", f"wada{b}"], w=[f"pmod{b}"], inc=(k == 7))
        c.op("dve", lambda: nc.vector.tensor_tensor(out=mrow[0:1, ch * 512:(ch + 1) * 512], in0=pm[b][0:1, :],
                                                    in1=brow[0:1, ch * 512:(ch + 1) * 512], op=ALU.add),
             r=[f"pmod{b}", "brow"], w=["mrow"])
    c.dma("sp", mod_d, mrow[:], r=["mrow"], w=["mod_d"])


def load_mod(c, mod_d, idf, st=None):
    nc = c.nc
    modT = c.sb("modT", [128, 72], F32, st)
    with contextlib.ExitStack() as s2:
        modR = c.sb("modR", [128, 128], F32, s2)
        pmt = c.ps("pmt", [128, 128], F32, s2)
        c.op("dve", lambda: nc.vector.memset(modR[:], 0.0), w=["modR"])
        c.dma("sp", modR[0:72, :], mod_d.rearrange("o (k p) -> (o k) p", p=128), r=["mod_d", "modR"], w=["modR"])
        c.op("pe", lambda: nc.tensor.transpose(out=pmt[:], in_=modR[:], identity=idf[:]), r=["modR", "idf"], w=["pmt"])
        c.op("dve", lambda: nc.vector.tensor_copy(out=modT[:], in_=pmt[:, 0:72]), r=["pmt"], w=["modT"])
        for s in (1, 4, 7):
            c.op("dve", lambda s=s: nc.vector.tensor_scalar_add(out=modT[:, s * 8:(s + 1) * 8], in0=modT[:, s * 8:(s + 1) * 8], scalar1=1.0),
                 r=["modT"], w=["modT"])
        c.barrier()
    return modT


def emit_norm_T(c, x_d, uT, ukey, modT, sh_i, sc_i, idb, st, uT_d=None):
    nc = c.nc
    xt = [c.sb(f"nx{i}", [128, D], F32, st) for i in range(2)]
    junk = c.sb("njunk", [128, D], F32, st)
    xn = [c.sb(f"nxn{i}", [128, D], BF16, st) for i in range(2)]
    stt = [c.sb(f"nst{i}", [128, 4], F32, st) for i in range(2)]
    ptr = [c.ps(f"ntr{i}", [128, 8, 128], BF16, st) for i in range(2)]
    for t in range(NTILE):
        b = t % 2
        c.dma("sp", xt[b][:], x_d[t * 128:(t + 1) * 128, :], r=["xsrc"], w=[f"nx{b}"])
        c.op("act", lambda: nc.scalar.activation(out=junk[:], in_=xt[b][:], func=AF.Square, accum_out=stt[b][:, 0:1]),
             r=[f"nx{b}"], w=["njunk", f"nst{b}"])
        c.op("act", lambda: nc.scalar.activation(out=stt[b][:, 1:2], in_=stt[b][:, 0:1], func=AF.Sqrt, scale=1.0 / D, bias=EPS),
             r=[f"nst{b}"], w=[f"nst{b}"])
        c.op("dve", lambda: nc.vector.reciprocal(out=stt[b][:, 2:3], in_=stt[b][:, 1:2]), r=[f"nst{b}"], w=[f"nst{b}"])
        c.op("act", lambda: nc.scalar.activation(out=xn[b][:], in_=xt[b][:], func=AF.Copy, scale=stt[b][:, 2:3]),
             r=[f"nx{b}", f"nst{b}"], w=[f"nxn{b}"])
        for k in range(8):
            c.op("pe", lambda k=k: nc.tensor.transpose(out=ptr[b][:, k, :], in_=xn[b][:, k * 128:(k + 1) * 128], identity=idb[:]),
                 r=[f"nxn{b}", "idb"], w=[f"ntr{b}"], inc=(k == 7))
        for k in range(8):
            dst = uT[:, k, t * 128:(t + 1) * 128]
            if k % 2 == 0:
                c.op("dve", lambda k=k, dst=dst: nc.vector.tensor_scalar(out=dst, in0=ptr[b][:, k, :],
                                                                         scalar1=modT[:, sc_i * 8 + k:sc_i * 8 + k + 1],
                                                                         scalar2=modT[:, sh_i * 8 + k:sh_i * 8 + k + 1],
                                                                         op0=ALU.mult, op1=ALU.add),
                     r=[f"ntr{b}", "modT"], w=[(ukey, t)])
            else:
                c.op("act", lambda k=k, dst=dst: nc.scalar.activation(out=dst, in_=ptr[b][:, k, :], func=AF.Identity,
                                                                      scale=modT[:, sc_i * 8 + k:sc_i * 8 + k + 1],
                                                                      bias=modT[:, sh_i * 8 + k:sh_i * 8 + k + 1]),
                     r=[f"ntr{b}", "modT"], w=[(ukey, t)])
    if uT_d is not None:
        c.dma("sp", uT_d, uT[:], r=[(ukey, t) for t in range(NTILE)], w=["uT_d"])


def emit_ffn(c, x_d, xo_d, modT, sh_i, sc_i, g_i, mod_d, wg_d, wu_d, wd_d, idb, xo_key="xo"):
    nc = c.nc
    with c.scope() as st:
        uT = c.sb("uT", [128, 8, NT], BF16, st)
        hT = c.sb("hT", [128, 22, NT], BF16, st)
        gbc = c.sb("gbc", [128, D], F32, st)
        c.dma("sp", gbc[:], mod_d[0:1, g_i * D:(g_i + 1) * D].partition_broadcast(128), r=["mod_d"], w=["gbc"])
        c.op("dve", lambda: nc.vector.tensor_scalar_mul(out=gbc[:], in0=gbc[:], scalar1=0.5), r=["gbc"], w=["gbc"])
        with contextlib.ExitStack() as st2:
            emit_norm_T(c, x_d, uT, "uT", modT, sh_i, sc_i, idb, st2)
            c.barrier()
        with contextlib.ExitStack() as st2:
            wg = [c.sb(f"wg{i}", [128, 8, 128], BF16, st2) for i in range(2)]
            wu = [c.sb(f"wu{i}", [128, 8, 128], BF16, st2) for i in range(2)]
            sg = [c.sb(f"sg{i}", [128, 512], BF16, st2) for i in range(2)]
            psg = [c.ps(f"psg{i}", [128, 512], F32, st2) for i in range(2)]
            psu = [c.ps(f"psu{i}", [128, 512], F32, st2) for i in range(2)]
            wgv = wg_d.rearrange("(k p) n -> p k n", p=128)
            wuv = wu_d.rearrange("(k p) n -> p k n", p=128)
            it = 0
            for f in range(22):
                b = f % 2
                c.dma("pool", wg[b][:], wgv[:, :, f * 128:(f + 1) * 128], w=[f"wg{b}"])
                c.dma("pool", wu[b][:], wuv[:, :, f * 128:(f + 1) * 128], w=[f"wu{b}"])
                for tg in range(4):
                    pb = it % 2
                    it += 1
                    ukeys = [("uT", 4 * tg + i) for i in range(4)]
                    for k in range(8):
                        c.op("pe", lambda k=k: nc.tensor.matmul(psg[pb][:], lhsT=wg[b][:, k, :], rhs=uT[:, k, tg * 512:(tg + 1) * 512],
                                                                start=(k == 0), stop=(k == 7)),
                             r=[f"wg{b}"] + ukeys, w=[f"psg{pb}"], inc=(k == 7))
                    for k in range(8):
                        c.op("pe", lambda k=k: nc.tensor.matmul(psu[pb][:], lhsT=wu[b][:, k, :], rhs=uT[:, k, tg * 512:(tg + 1) * 512],
                                                                start=(k == 0), stop=(k == 7)),
                             r=[f"wu{b}"] + ukeys, w=[f"psu{pb}"], inc=(k == 7))
                    c.op("act", lambda: nc.scalar.activation(out=sg[pb][:], in_=psg[pb][:], func=AF.Silu),
                         r=[f"psg{pb}"], w=[f"sg{pb}"])
                    c.op("dve", lambda: nc.vector.tensor_tensor(out=hT[:, f, tg * 512:(tg + 1) * 512], in0=sg[pb][:], in1=psu[pb][:], op=ALU.mult),
                         r=[f"sg{pb}", f"psu{pb}"], w=[("hT", f, tg)])
            c.barrier()
        with contextlib.ExitStack() as st2:
            wd = [c.sb(f"wd{i}", [128, 22, 512], BF16, st2) for i in range(2)]
            xh = [c.sb(f"xh{i}", [128, 512], F32, st2) for i in range(2)]
            tmp = [c.sb(f"tmp{i}", [128, 512], F32, st2) for i in range(2)]
            pso = [c.ps(f"pso{i}", [128, 512], F32, st2) for i in range(2)]
            wdv = wd_d.rearrange("(f p) n -> p f n", p=128)
            it = 0
            for half in range(2):
                for f0 in range(0, 22, 2):
                    c.dma("pool", wd[half][:, f0:f0 + 2, :], wdv[:, f0:f0 + 2, half * 512:(half + 1) * 512], w=[f"wd{half}"])
                for t in range(NTILE):
                    pb = it % 2
                    it += 1
                    c.dma("sp", xh[pb][:], x_d[t * 128:(t + 1) * 128, half * 512:(half + 1) * 512], r=["xsrc"], w=[f"xh{pb}"])
                    for f in range(22):
                        c.op("pe", lambda f=f: nc.tensor.matmul(pso[pb][:], lhsT=hT[:, f, t * 128:(t + 1) * 128], rhs=wd[half][:, f, :],
                                                                start=(f == 0), stop=(f == 21)),
                             r=[f"wd{half}", ("hT", f, t // 4)], w=[f"pso{pb}"], inc=(f == 21))
                    c.op("dve", lambda: nc.vector.tensor_tensor(out=tmp[pb][:], in0=pso[pb][:], in1=gbc[:, half * 512:(half + 1) * 512], op=ALU.mult),
                         r=[f"pso{pb}", "gbc"], w=[f"tmp{pb}"])
                    c.op("pool", lambda: nc.gpsimd.tensor_tensor(out=tmp[pb][:], in0=tmp[pb][:], in1=xh[pb][:], op=ALU.add),
                         r=[f"tmp{pb}", f"xh{pb}"], w=[f"tmp{pb}"])
                    c.dma("sp", xo_d[t * 128:(t + 1) * 128, half * 512:(half + 1) * 512], tmp[pb][:], r=[f"tmp{pb}"], w=[xo_key])
            c.barrier()


OFF_QA, OFF_KC, OFF_VC, OFF_KS, OFF_VS, OFF_KW, OFF_VW, OFF_GN, OFF_CQ, OFF_CKV, OFF_KR, OFF_GM = (
    0, 512, 640, 768, 896, 1024, 1152, 1280, 1304, 1688, 1944, 1976)


def emit_proj_fm(c, u2T, w_in_d, QA_d, KX_d):
    nc = c.nc
    wv = w_in_d.rearrange("(k p) n -> p k n", p=128)
    with c.scope() as st:
        wq = [c.sb(f"wq{i}", [128, 8, 128], BF16, st) for i in range(2)]
        qa_sb = c.sb("qa_sb", [64, 8, NT], BF16, st)
        kx_sb = c.sb("kx_sb", [128, 4, NT], BF16, st)
        pq = [c.ps(f"pq{i}", [128, 512], F32, st) for i in range(2)]
        blocks = [(OFF_QA + 64 * h, 64, "q", h) for h in range(8)] + \
                 [(OFF_KC, 128, "k", 0), (OFF_VC, 128, "k", 1), (OFF_KS, 128, "k", 2), (OFF_KW, 128, "k", 3)]
        it = 0
        for bi, (col0, M, kind, idx) in enumerate(blocks):
            b = bi % 2
            c.dma("pool", wq[b][:, :, 0:M], wv[:, :, col0:col0 + M], w=[f"wq{b}"])
            for tg in range(4):
                pb = it % 2
                it += 1
                ukeys = [("u2T", 4 * tg + i) for i in range(4)]
                for k in range(8):
                    c.op("pe", lambda k=k: nc.tensor.matmul(pq[pb][0:M, :], lhsT=wq[b][:, k, 0:M], rhs=u2T[:, k, tg * 512:(tg + 1) * 512],
                                                            start=(k == 0), stop=(k == 7)),
                         r=[f"wq{b}"] + ukeys, w=[f"pq{pb}"], inc=(k == 7))
                if kind == "q":
                    dst = qa_sb[0:64, idx, tg * 512:(tg + 1) * 512]
                    key = "qa_sb"
                else:
                    dst = kx_sb[:, idx, tg * 512:(tg + 1) * 512]
                    key = "kx_sb"
                if it % 2 == 0:
                    c.op("act", lambda dst=dst: nc.scalar.copy(out=dst, in_=pq[pb][0:M, :]), r=[f"pq{pb}"], w=[key])
                else:
                    c.op("dve", lambda dst=dst: nc.vector.tensor_copy(out=dst, in_=pq[pb][0:M, :]), r=[f"pq{pb}"], w=[key])
        c.dma("sp", QA_d, qa_sb[:], r=["qa_sb"], w=["QA_d"])
        c.dma("sp", KX_d, kx_sb[:], r=["kx_sb"], w=["KX_d"])


def emit_rms_rows(c, src_ap, n, stt, col, key_r, key_st, junk):
    nc = c.nc
    c.op("act", lambda: nc.scalar.activation(out=junk[:, 0:n], in_=src_ap, func=AF.Square, accum_out=stt[:, col:col + 1]),
         r=[key_r], w=["junk", key_st])
    c.op("act", lambda: nc.scalar.activation(out=stt[:, col + 1:col + 2], in_=stt[:, col:col + 1], func=AF.Sqrt, scale=1.0 / n, bias=EPS),
         r=[key_st], w=[key_st])
    c.op("dve", lambda: nc.vector.reciprocal(out=stt[:, col + 2:col + 3], in_=stt[:, col + 1:col + 2]), r=[key_st], w=[key_st])


def emit_proj_tm(c, u2T, w_in_d, wuq_d, qnT_d, cs8_d, idb, VSW_d, GATES_d, CKVT_d, KRT_d, QMT_d):
    nc = c.nc
    wv = w_in_d.rearrange("(k p) n -> p k n", p=128)
    with c.scope() as st:
        wvs = c.sb("wvs", [128, 8, 256], BF16, st)
        wr = c.sb("wr", [128, 8, 696], BF16, st)
        wuq = c.sb("wuq", [128, 3, 768], BF16, st)
        qnT = c.sb("qnT", [128, 3], F32, st)
        c.dma("pool", wvs[:, :, 0:128], wv[:, :, OFF_VS:OFF_VS + 128], w=["wvs"])
        c.dma("pool", wvs[:, :, 128:256], wv[:, :, OFF_VW:OFF_VW + 128], w=["wvs"])
        for k0 in range(0, 8, 2):
            c.dma("pool", wr[:, k0:k0 + 2, :], wv[:, k0:k0 + 2, OFF_GN:OFF_GM], w=["wr"])
        c.dma("pool", wuq[:], wuq_d.rearrange("(c p) n -> p c n", p=128), w=["wuq"])
        c.dma("sp", qnT[:], qnT_d, w=["qnT"])
        vsw_sb = c.sb("vsw_sb", [128, NTILE, 256], BF16, st)
        gates_sb = c.sb("gates_sb", [128, NTILE, 24], F32, st)
        ckvT_sb = c.sb("ckvT_sb", [128, 2, NT], BF16, st)
        krT_sb = c.sb("krT_sb", [32, NT], BF16, st)
        qmT_sb = c.sb("qmT_sb", [96, 8, NT], BF16, st)
        cs = [c.sb(f"cs{i}", [128, 256], F32, st) for i in range(2)]
        junk = c.sb("junk", [128, 512], F32, st)
        stt = c.sb("stt", [128, 8], F32, st)
        cqn = c.sb("cqn", [128, 384], BF16, st)
        cqnT = c.sb("cqnT", [128, 3, 128], BF16, st)
        qm = c.sb("qm", [128, 768], F32, st)
        qmb = c.sb("qmb", [128, 8, 96], BF16, st)
        tt = [c.sb(f"tt{i}", [128, 8, 16], F32, st) for i in range(4)]
        ckvn = c.sb("ckvn", [128, 256], BF16, st)
        kk = [c.sb(f"kk{i}", [128, 16], F32, st) for i in range(4)]
        krb = c.sb("krb", [128, 32], BF16, st)
        pv = c.ps("pv", [128, 256], F32, st)
        pr1 = c.ps("pr1", [128, 408], F32, st)
        pr2 = c.ps("pr2", [128, 288], F32, st)
        pq1 = c.ps("pq1", [128, 512], F32, st)
        pq2 = c.ps("pq2", [128, 256], F32, st)
        ptr = c.ps("ptr", [128, 8, 128], BF16, st)
        ptr2 = c.ps("ptr2", [128, 8, 128], BF16, st)
        qm3 = qm[:].rearrange("p (h d) -> p h d", d=96)
        for t in range(NTILE):
            b = t % 2
            tsl = slice(t * 128, (t + 1) * 128)
            c.dma("sp", cs[b][:], cs8_d[tsl, :], w=[f"cs{b}"])
            cos8 = cs[b][:, 0:128].rearrange("p (h d) -> p h d", d=16)
            sin8 = cs[b][:, 128:256].rearrange("p (h d) -> p h d", d=16)
            for (ps_, rhs_fn, key) in ((pv, lambda k: wvs[:, k, :], "pv"), (pr1, lambda k: wr[:, k, 0:408], "pr1"),
                                       (pr2, lambda k: wr[:, k, 408:696], "pr2")):
                for k in range(8):
                    c.op("pe", lambda k=k, ps_=ps_, rhs_fn=rhs_fn: nc.tensor.matmul(ps_[:], lhsT=u2T[:, k, tsl], rhs=rhs_fn(k),
                                                                                  start=(k == 0), stop=(k == 7)),
                         r=[("u2T", t), "wvs", "wr"], w=[key], inc=(k == 7))
            c.op("act", lambda: nc.scalar.copy(out=vsw_sb[:, t, :], in_=pv[:]), r=["pv"], w=["vsw_sb"])
            c.op("act", lambda: nc.scalar.activation(out=gates_sb[:, t, :], in_=pr1[:, 0:24], func=AF.Sigmoid), r=["pr1"], w=["gates_sb"])
            emit_rms_rows(c, pr1[:, 24:408], 384, stt, 0, "pr1", "stt", junk)
            c.op("act", lambda: nc.scalar.activation(out=cqn[:], in_=pr1[:, 24:408], func=AF.Copy, scale=stt[:, 2:3]),
                 r=["pr1", "stt"], w=["cqn"])
            for cc in range(3):
                c.op("pe", lambda cc=cc: nc.tensor.transpose(out=ptr[:, cc, :], in_=cqn[:, cc * 128:(cc + 1) * 128], identity=idb[:]),
                     r=["cqn", "idb"], w=["ptr"], inc=(cc == 2))
            for cc in range(3):
                c.op("dve", lambda cc=cc: nc.vector.tensor_scalar(out=cqnT[:, cc, :], in0=ptr[:, cc, :], scalar1=qnT[:, cc:cc + 1],
                                                                   scalar2=None, op0=ALU.mult),
                     r=["ptr", "qnT"], w=["cqnT"])
            for cc in range(3):
                c.op("pe", lambda cc=cc: nc.tensor.matmul(pq1[:], lhsT=cqnT[:, cc, :], rhs=wuq[:, cc, 0:512], start=(cc == 0), stop=(cc == 2)),
                     r=["cqnT", "wuq"], w=["pq1"], inc=(cc == 2))
            for cc in range(3):
                c.op("pe", lambda cc=cc: nc.tensor.matmul(pq2[:], lhsT=cqnT[:, cc, :], rhs=wuq[:, cc, 512:768], start=(cc == 0), stop=(cc == 2)),
                     r=["cqnT", "wuq"], w=["pq2"], inc=(cc == 2))
            c.op("act", lambda: nc.scalar.copy(out=qm[:, 0:512], in_=pq1[:]), r=["pq1"], w=["qm"])
            c.op("dve", lambda: nc.vector.tensor_copy(out=qm[:, 512:768], in_=pq2[:]), r=["pq2"], w=["qm"])
            x1 = qm3[:, :, 64:80]
            x2 = qm3[:, :, 80:96]
            c.op("dve", lambda: nc.vector.tensor_tensor(out=tt[0][:], in0=x1, in1=cos8, op=ALU.mult), r=["qm", f"cs{b}"], w=["tt0"])
            c.op("pool", lambda: nc.gpsimd.tensor_tensor(out=tt[1][:], in0=x2, in1=sin8, op=ALU.mult), r=["qm", f"cs{b}"], w=["tt1"])
            c.op("dve", lambda: nc.vector.tensor_tensor(out=tt[2][:], in0=x1, in1=sin8, op=ALU.mult), r=["qm", f"cs{b}"], w=["tt2"])
            c.op("pool", lambda: nc.gpsimd.tensor_tensor(out=tt[3][:], in0=x2, in1=cos8, op=ALU.mult), r=["qm", f"cs{b}"], w=["tt3"])
            c.op("dve", lambda: nc.vector.tensor_tensor(out=qmb[:, :, 64:80], in0=tt[0][:], in1=tt[1][:], op=ALU.subtract),
                 r=["tt0", "tt1"], w=["qmb"])
            c.op("dve", lambda: nc.vector.tensor_tensor(out=qmb[:, :, 80:96], in0=tt[2][:], in1=tt[3][:], op=ALU.add),
                 r=["tt2", "tt3"], w=["qmb"])
            c.op("act", lambda: nc.scalar.copy(out=qmb[:, :, 0:64], in_=qm3[:, :, 0:64]), r=["qm"], w=["qmb"])
            for h in range(8):
                c.op("pe", lambda h=h: nc.tensor.transpose(out=ptr2[0:96, h, :], in_=qmb[:, h, :], identity=idb[:]),
                     r=["qmb", "idb"], w=["ptr2"], inc=(h == 7))
            c.op("act", lambda: nc.scalar.copy(out=qmT_sb[:, :, tsl], in_=ptr2[0:96, :, :]), r=["ptr2"], w=["qmT_sb"])
            emit_rms_rows(c, pr2[:, 0:256], 256, stt, 3, "pr2", "stt", junk)
            c.op("act", lambda: nc.scalar.activation(out=ckvn[:], in_=pr2[:, 0:256], func=AF.Copy, scale=stt[:, 5:6]),
                 r=["pr2", "stt"], w=["ckvn"])
            for cc in range(2):
                c.op("pe", lambda cc=cc: nc.tensor.transpose(out=ptr[:, 3 + cc, :], in_=ckvn[:, cc * 128:(cc + 1) * 128], identity=idb[:]),
                     r=["ckvn", "idb"], w=["ptr"], inc=(cc == 1))
            c.op("dve", lambda: nc.vector.tensor_copy(out=ckvT_sb[:, :, tsl], in_=ptr[:, 3:5, :]), r=["ptr"], w=["ckvT_sb"])
            kx1 = pr2[:, 256:272]
            kx2 = pr2[:, 272:288]
            cos16 = cs[b][:, 0:16]
            sin16 = cs[b][:, 128:144]
            c.op("dve", lambda: nc.vector.tensor_tensor(out=kk[0][:], in0=kx1, in1=cos16, op=ALU.mult), r=["pr2", f"cs{b}"], w=["kk0"])
            c.op("dve", lambda: nc.vector.tensor_tensor(out=kk[1][:], in0=kx2, in1=sin16, op=ALU.mult), r=["pr2", f"cs{b}"], w=["kk1"])
            c.op("dve", lambda: nc.vector.tensor_tensor(out=kk[2][:], in0=kx1, in1=sin16, op=ALU.mult), r=["pr2", f"cs{b}"], w=["kk2"])
            c.op("dve", lambda: nc.vector.tensor_tensor(out=kk[3][:], in0=kx2, in1=cos16, op=ALU.mult), r=["pr2", f"cs{b}"], w=["kk3"])
            c.op("dve", lambda: nc.vector.tensor_tensor(out=krb[:, 0:16], in0=kk[0][:], in1=kk[1][:], op=ALU.subtract), r=["kk0", "kk1"], w=["krb"])
            c.op("dve", lambda: nc.vector.tensor_tensor(out=krb[:, 16:32], in0=kk[2][:], in1=kk[3][:], op=ALU.add), r=["kk2", "kk3"], w=["krb"])
            c.op("pe", lambda: nc.tensor.transpose(out=ptr[0:32, 5, :], in_=krb[:], identity=idb[:]), r=["krb", "idb"], w=["ptr"])
            c.op("dve", lambda: nc.vector.tensor_copy(out=krT_sb[:, tsl], in_=ptr[0:32, 5, :]), r=["ptr"], w=["krT_sb"])
        c.dma("sp", VSW_d.rearrange("(t p) n -> p t n", p=128), vsw_sb[:], r=["vsw_sb"], w=["VSW_d"])
        c.dma("sp", GATES_d.rearrange("(t p) n -> p t n", p=128), gates_sb[:], r=["gates_sb"], w=["GATES_d"])
        c.dma("sp", CKVT_d, ckvT_sb[:], r=["ckvT_sb"], w=["CKVT_d"])
        c.dma("sp", KRT_d, krT_sb[:], r=["krT_sb"], w=["KRT_d"])
        c.dma("sp", QMT_d, qmT_sb[:], r=["qmT_sb"], w=["QMT_d"])


SLOPES = [2.0 ** (-(h + 1)) for h in range(8)]


def core_positions(r):
    n = np.arange(NT)
    return 512 * (4 * (n // 512) + r) + n % 512


def build_A(with_mod=True):
    nc = bass.Bass("TRN2", target_bir_lowering=False)
    di = lambda name, shape, dt=F32: nc.dram_tensor(name, list(shape), dt, kind="ExternalInput").ap()
    do = lambda name, shape, dt=F32: nc.dram_tensor(name, list(shape), dt, kind="ExternalOutput").ap()
    x_d = di("xin", [NT, D])
    ident_d = di("ident", [128, 128])
    wg_d = di("wg", [D, DFF]); wu_d = di("wu", [D, DFF]); wd_d = di("wd", [DFF, D])
    w_in_d = di("w_in", [D, 4024])
    wuq_d = di("wuq", [384, 768])
    qnT_d = di("qnT", [128, 3])
    cs8_d = di("cs8", [NT, 256])
    if with_mod:
        cT_d = di("cT", [128, 8]); wada_d = di("w_ada", [D, 9216]); bada_d = di("b_ada", [1, 9216])
        mod_d = do("mod", [1, 9216])
    else:
        mod_d = di("mod", [1, 9216])
    x1_d = do("x1", [NT, D])
    U2T_d = do("U2T", [128, 8, NT], BF16)
    QA_d = do("QA", [64, 8, NT], BF16)
    KX_d = do("KX", [128, 4, NT], BF16)
    VSW_d = do("VSW", [NT, 256], BF16)
    GATES_d = do("GATES", [NT, 24])
    CKVT_d = do("CKVT", [128, 2, NT], BF16)
    KRT_d = do("KRT", [32, NT], BF16)
    QMT_d = do("QMT", [96, 8, NT], BF16)
    with contextlib.ExitStack() as st:
        c = Ctx(nc, st)
        idb, idf = emit_consts(c, ident_d)
        if with_mod:
            with c.scope() as s1:
                emit_mod(c, cT_d, wada_d, bada_d, mod_d, s1)
        modT = load_mod(c, mod_d, idf)
        emit_ffn(c, x_d, x1_d, modT, 0, 1, 2, mod_d, wg_d, wu_d, wd_d, idb)
        u2T = c.sb("u2T", [128, 8, NT], BF16)
        with c.scope() as s2:
            emit_norm_T(c, x1_d, u2T, "u2T", modT, 3, 4, idb, s2, uT_d=U2T_d)
        emit_proj_fm(c, u2T, w_in_d, QA_d, KX_d)
        emit_proj_tm(c, u2T, w_in_d, wuq_d, qnT_d, cs8_d, idb, VSW_d, GATES_d, CKVT_d, KRT_d, QMT_d)
        c.barrier()
    return nc


def rope_tables(pos):
    half = 16
    inv_freq = (10000.0 ** (-np.arange(half, dtype=np.float32) / half)).astype(np.float32)
    ang = pos.astype(np.float32)[:, None] * inv_freq[None, :]
    cos = np.cos(ang).astype(np.float32); sin = np.sin(ang).astype(np.float32)
    return np.concatenate([cos] * 8 + [sin] * 8, axis=1).astype(np.float32)


def inputs_A(inp, l, cid, x_core, mod=None):
    b, r = cid // 4, cid % 4
    pos = core_positions(r)
    d = {"xin": x_core, "ident": np.eye(128, dtype=np.float32),
         "wg": inp["ffn1_gate"][l], "wu": inp["ffn1_up"][l], "wd": inp["ffn1_down"][l],
         "w_in": inp["w_in"][l], "wuq": inp["mla_w_uq"][l],
         "qnT": np.ascontiguousarray(inp["mla_q_norm"][l].reshape(3, 128).T),
         "cs8": rope_tables(pos)}
    if mod is None:
        d.update({"cT": np.ascontiguousarray(inp["c"][b].reshape(8, 128).T), "w_ada": inp["w_ada"][l], "b_ada": inp["b_ada"][l][None]})
    else:
        d["mod"] = mod
    return d


SCALE_NSA = 0.125
SCALE_MLA = 96.0 ** -0.5
JB = [0, 16, 48, 96]


def host_consts_B(r):
    PAD = (3 - r) * 512
    n = np.arange(NT)
    qpos = 512 * (4 * (n // 512) + 3) + n % 512
    d = {}
    d["QPOS"] = np.ascontiguousarray(qpos.reshape(16, 128).T.astype(np.float32))
    m = np.arange(512)
    cend = (16 * m + 31).astype(np.float32)
    cend[511] = 1e9
    d["CEND"] = cend[None, :]
    slopes = np.array(SLOPES, np.float32)
    cb = slopes[:, None] * cend[None, :]
    cb[:, 511] = 0.0
    cb = cb + np.where(16 * m < PAD, -1e9, 0.0)[None, :]
    d["CB"] = cb.astype(np.float32)
    p = np.arange(128)
    kb = np.arange(64)
    d["KVALID"] = (128 * kb[None, :] + p[:, None] >= PAD).astype(np.float32)
    d["RV"] = (d["QPOS"] - PAD >= 31).astype(np.float32)
    jj = np.arange(128)
    MB = np.zeros((16, 128, 128), np.float32)
    for qt in range(16):
        t = d["QPOS"][:, qt].astype(np.int64)
        cur = t // 64
        valid = (jj[None, :] <= cur[:, None]) & (jj[None, :] >= PAD // 64)
        forced = ((jj[None, :] == PAD // 64) | (jj[None, :] == cur[:, None]) | (jj[None, :] == cur[:, None] - 1)) & valid
        MB[qt] = np.where(forced, 1e4, np.where(valid, 0.0, -1e4))
    d["MB"] = MB
    kbias = np.zeros((128, 160, 8), np.float32)
    for j in range(4):
        for k in range(16 * j + 16):
            kbias[:, JB[j] + k, :] = slopes[None, :] * (128 * k + p[:, None] - 512 * (4 * j + 3))
    d["KBIAS"] = kbias.reshape(128, 1280)
    d["COLB"] = (-slopes[:, None] * (n % 512)[None, :] / SCALE_NSA).astype(np.float32)
    c = np.arange(512)
    d["DMASK"] = np.stack([(128 * di + p[:, None] <= c[None, :]) for di in range(4)]).astype(np.float32)
    wm = []
    for i in range(8):
        dd = c[None, :] - (128 * i + p[:, None] - 512)
        wm.append((dd >= 0) & (dd < 512))
    d["WMASK"] = np.stack(wm).astype(np.float32)
    d["EALL"] = (jj[:, None, None] == 2 * kb[None, :, None] + (p[None, None, :] // 64)).astype(np.float32)
    d["ident"] = np.eye(128, dtype=np.float32)
    return d


def to_view(arr, axis, r):
    PAD = (3 - r) * 512
    if PAD == 0:
        return np.ascontiguousarray(arr)
    shp = list(arr.shape)
    shp[axis] = PAD
    z = np.zeros(shp, arr.dtype)
    sl = [slice(None)] * arr.ndim
    sl[axis] = slice(0, 8192 - PAD)
    return np.ascontiguousarray(np.concatenate([z, arr[tuple(sl)]], axis=axis))


def emit_compress(c, KXV_d, g, which, w1_d, w2_d, peT_d, out_kT, out_vtok, st_outer):
    nc = c.nc
    with c.scope() as st:
        kc = c.sb("kc", [64, 8192], BF16, st)
        c.dma("sp", kc[:], KXV_d[g * 64:(g + 1) * 64, which, :], w=["kc"])
        w1 = c.sb("w1", [64, 32, 64], BF16, st)
        c.dma("pool", w1[:], w1_d.rearrange("(t d) o -> d t o", d=64), w=["w1"])
        w2 = c.sb("w2", [64, 64], BF16, st)
        c.dma("pool", w2[:], w2_d, w=["w2"])
        peT = c.sb("peT", [64, 32], BF16, st)
        c.dma("pool", peT[:], peT_d, w=["peT"])
        pb = c.ps("pb", [64, 1], F32, st)
        for t in range(32):
            c.op("pe", lambda t=t: nc.tensor.matmul(pb[:], lhsT=w1[:, t, :], rhs=peT[:, t:t + 1], start=(t == 0), stop=(t == 31)),
                 r=["w1", "peT"], w=["pb"], inc=(t == 31))
        b1 = c.sb("b1", [64, 1], F32, st)
        c.op("dve", lambda: nc.vector.tensor_copy(out=b1[:], in_=pb[:]), r=["pb"], w=["b1"])
        ph = c.ps("ph", [64, 512], F32, st)
        kc3 = kc[:].rearrange("p (m s) -> p m s", s=16)
        for t in range(32):
            a = t // 16
            c.op("pe", lambda t=t, a=a: nc.tensor.matmul(ph[:, 0:511], lhsT=w1[:, t, :], rhs=kc3[:, a:a + 511, t % 16],
                                                         start=(t == 0), stop=(t == 31)),
                 r=["w1", "kc"], w=["ph"], inc=(t == 31))
        xg = c.sb("xg", [64, 512], F32, st)
        x2 = c.sb("x2", [64, 512], F32, st)
        gg = c.sb("gg", [64, 512], BF16, st)
        c.op("dve", lambda: nc.vector.memset(xg[:], 0.0), w=["xg"])
        c.op("act", lambda: nc.scalar.activation(out=xg[:, 0:511], in_=ph[:, 0:511], func=AF.Identity, bias=b1[:, 0:1]),
             r=["ph", "b1", "xg"], w=["xg"])
        c.op("dve", lambda: nc.vector.tensor_tensor(out=x2[:], in0=xg[:], in1=xg[:], op=ALU.mult), r=["xg"], w=["x2"])
        c.op("dve", lambda: nc.vector.tensor_scalar(out=x2[:], in0=x2[:], scalar1=0.044715, scalar2=1.0, op0=ALU.mult, op1=ALU.add),
             r=["x2"], w=["x2"])
        c.op("dve", lambda: nc.vector.tensor_tensor(out=x2[:], in0=x2[:], in1=xg[:], op=ALU.mult), r=["x2", "xg"], w=["x2"])
        c.op("act", lambda: nc.scalar.activation(out=x2[:], in_=x2[:], func=AF.Tanh, scale=0.7978845608028654), r=["x2"], w=["x2"])
        c.op("dve", lambda: nc.vector.tensor_scalar(out=x2[:], in0=x2[:], scalar1=0.5, scalar2=0.5, op0=ALU.mult, op1=ALU.add),
             r=["x2"], w=["x2"])
        c.op("dve", lambda: nc.vector.tensor_tensor(out=gg[:], in0=x2[:], in1=xg[:], op=ALU.mult), r=["x2", "xg"], w=["gg"])
        if which == 0:
            po = c.ps("po", [64, 512], F32, st)
            c.op("pe", lambda: nc.tensor.matmul(po[:], lhsT=w2[:], rhs=gg[:], start=True, stop=True), r=["w2", "gg"], w=["po"])
            c.op("dve", lambda: nc.vector.tensor_copy(out=out_kT[:], in_=po[:]), r=["po"], w=[("kcT", g)])
        else:
            po = c.ps("po", [128, 4, 64], F32, st)
            c.op("dve", lambda: nc.vector.memset(out_vtok[:], 0.0), w=[("vc", g)])
            for ck in range(4):
                M = 128 if ck < 3 else 127
                c.op("pe", lambda ck=ck, M=M: nc.tensor.matmul(po[0:M, ck, :], lhsT=gg[:, ck * 128:ck * 128 + M], rhs=w2[:], start=True, stop=True),
                     r=["w2", "gg"], w=["po"], inc=(ck == 3))
            c.op("dve", lambda: nc.vector.tensor_copy(out=out_vtok[:, 0:3, :], in_=po[:, 0:3, :]), r=["po"], w=[("vc", g)])
            c.op("dve", lambda: nc.vector.tensor_copy(out=out_vtok[0:127, 3, :], in_=po[0:127, 3, :]), r=["po"], w=[("vc", g)])


def emit_o_epilogue(c, oT_ps, okey, o_sb, okey_sb, h, j, gates_sb, gcol, first, idf, tmp, ptO, st_tiles):
    nc = c.nc
    oT_sb, rz = st_tiles
    c.op("act", lambda: nc.scalar.copy(out=oT_sb[:], in_=oT_ps[:]), r=[okey], w=["oT_sb"])
    for tt in range(4):
        c.op("pe", lambda tt=tt: nc.tensor.transpose(out=ptO[:, tt, 0:65], in_=oT_sb[:, tt * 128:(tt + 1) * 128], identity=idf[0:65, 0:65]),
             r=["oT_sb", "idf"], w=["ptO"], inc=(tt == 3))
    for tt in range(4):
        qt = 4 * j + tt
        c.op("dve", lambda tt=tt: nc.vector.reciprocal(out=rz[:, tt:tt + 1], in_=ptO[:, tt, 64:65]), r=["ptO"], w=["rz"])
        if gcol is not None:
            c.op("dve", lambda tt=tt, qt=qt: nc.vector.tensor_tensor(out=rz[:, tt:tt + 1], in0=rz[:, tt:tt + 1],
                                                                    in1=gates_sb[:, qt, gcol:gcol + 1], op=ALU.mult),
                 r=["rz", "gates_sb"], w=["rz"])
        dst = o_sb[:, qt, h * 64:(h + 1) * 64]
        if first:
            c.op("dve", lambda tt=tt, dst=dst: nc.vector.tensor_scalar(out=dst, in0=ptO[:, tt, 0:64], scalar1=rz[:, tt:tt + 1], scalar2=None,
                                                                       op0=ALU.mult),
                 r=["ptO", "rz"], w=[(okey_sb, qt)])
        else:
            c.op("dve", lambda tt=tt, dst=dst: nc.vector.scalar_tensor_tensor(out=dst, in0=ptO[:, tt, 0:64], scalar=rz[:, tt:tt + 1], in1=dst,
                                                                              op0=ALU.mult, op1=ALU.add),
                 r=["ptO", "rz", (okey_sb, qt)], w=[(okey_sb, qt)])


def emit_cmp_select(c, qa_d, colb_d, gates_sb, o_a_sb, selT, kcT, vct, consts, idf):
    nc = c.nc
    QPOS, RV = consts["QPOS"], consts["RV"]
    with c.scope() as st:
        qa = c.sb("qa_all", [64, 8, NT], BF16, st)
        c.dma("sp", qa[:], qa_d, w=["qa_all"])
        cend_bc = c.sb("cend_bc", [128, 512], F32, st)
        c.dma("sp", cend_bc[:], consts["CEND_d"][0:1, :].partition_broadcast(128), w=["cend_bc"])
        cb_bc = c.sb("cb_bc", [128, 8, 512], F32, st)
        for h in range(8):
            c.dma("sp", cb_bc[:, h, :], consts["CB_d"][h:h + 1, :].partition_broadcast(128), w=["cb_bc"])
        mbt = [c.sb(f"mbt{i}", [128, 128], F32, st) for i in range(2)]
        maskb = c.sb("maskb", [128, 512], F32, st)
        BM = [c.sb(f"BM{i}", [128, 512], F32, st) for i in range(2)]
        tsb = [c.sb(f"tsb{i}", [128, 512], F32, st) for i in range(2)]
        esb = [c.sb(f"esb{i}", [128, 512], F32, st) for i in range(2)]
        peT = [c.sb(f"peT{i}", [128, 4, 128], BF16, st) for i in range(2)]
        sm = [c.sb(f"sm{i}", [128, 8], F32, st) for i in range(2)]
        imp = c.sb("imp", [128, 520], F32, st)
        impS = c.sb("impS", [128, 128], F32, st)
        imp2 = c.sb("imp2", [128, 128], F32, st)
        m8 = c.sb("m8", [128, 16], F32, st)
        sel = c.sb("sel", [128, 128], F32, st)
        pS = [c.ps(f"pS{i}", [128, 512], F32, st) for i in range(2)]
        pT = c.ps("pT", [128, 4, 128], F32, st)
        pOc = c.ps("pOc", [128, 4, 64], F32, st)
        pSel = c.ps("pSel", [128, 128], F32, st)
        imp4 = imp[:].rearrange("p (j s) -> p j s", s=4)
        it = 0
        for qt in range(NTILE):
            qsl = slice(qt * 128, (qt + 1) * 128)
            q0 = float(512 * (4 * (qt // 4) + 3) + 128 * (qt % 4))
            c.dma("sp", mbt[qt % 2][:], consts["MB_d"][qt], w=[f"mbt{qt % 2}"])
            c.op("dve", lambda: nc.vector.tensor_scalar(out=maskb[:], in0=cend_bc[:], scalar1=QPOS[:, qt:qt + 1], scalar2=-1e9,
                                                        op0=ALU.is_gt, op1=ALU.mult),
                 r=["cend_bc", "QPOS"], w=["maskb"])
            for g in range(2):
                c.op("pool", lambda: nc.gpsimd.memset(imp[:], 0.0), w=["imp"])
                for hh in range(4):
                    h = 4 * g + hh
                    b = it % 2
                    it += 1
                    c.op("pe", lambda: nc.tensor.matmul(pS[b][:], lhsT=qa[:, h, qsl], rhs=kcT[g][:], start=True, stop=True),
                         r=["qa_all", ("kcT", g)], w=[f"pS{b}"])
                    c.op("dve", lambda: nc.vector.scalar_tensor_tensor(out=BM[b][:], in0=cb_bc[:, h, :], scalar=-SLOPES[h] * q0, in1=maskb[:],
                                                                       op0=ALU.add, op1=ALU.add),
                         r=["cb_bc", "maskb"], w=[f"BM{b}"])
                    c.op("dve", lambda: nc.vector.scalar_tensor_tensor(out=tsb[b][:], in0=pS[b][:], scalar=SCALE_NSA, in1=BM[b][:],
                                                                       op0=ALU.mult, op1=ALU.add),
                         r=[f"pS{b}", f"BM{b}"], w=[f"tsb{b}"])
                    c.op("dve", lambda: nc.vector.tensor_reduce(out=sm[b][:, 0:1], in_=tsb[b][:], axis=AX.X, op=ALU.max, negate=True),
                         r=[f"tsb{b}"], w=[f"sm{b}"])
                    c.op("act", lambda: nc.scalar.activation(out=esb[b][:], in_=tsb[b][:], func=AF.Exp, bias=sm[b][:, 0:1],
                                                             accum_out=sm[b][:, 1:2]),
                         r=[f"tsb{b}", f"sm{b}"], w=[f"esb{b}", f"sm{b}"])
                    c.op("dve", lambda: nc.vector.reciprocal(out=sm[b][:, 2:3], in_=sm[b][:, 1:2]), r=[f"sm{b}"], w=[f"sm{b}"])
                    c.op("dve", lambda: nc.vector.tensor_tensor(out=sm[b][:, 3:4], in0=sm[b][:, 2:3], in1=RV[:, qt:qt + 1], op=ALU.mult),
                         r=[f"sm{b}", "RV"], w=[f"sm{b}"])
                    c.op("dve", lambda: nc.vector.tensor_tensor(out=sm[b][:, 4:5], in0=sm[b][:, 3:4], in1=gates_sb[:, qt, 3 * h:3 * h + 1], op=ALU.mult),
                         r=[f"sm{b}", "gates_sb"], w=[f"sm{b}"])
                    if hh == 0:
                        c.op("dve", lambda: nc.vector.tensor_scalar(out=imp[:, 1:513], in0=esb[b][:], scalar1=sm[b][:, 3:4], scalar2=None, op0=ALU.mult),
                             r=[f"esb{b}", f"sm{b}", "imp"], w=["imp"])
                    else:
                        c.op("dve", lambda: nc.vector.scalar_tensor_tensor(out=imp[:, 1:513], in0=esb[b][:], scalar=sm[b][:, 3:4], in1=imp[:, 1:513],
                                                                           op0=ALU.mult, op1=ALU.add),
                             r=[f"esb{b}", f"sm{b}", "imp"], w=["imp"])
                    for ck in range(4):
                        c.op("pe", lambda ck=ck: nc.tensor.transpose(out=pT[:, ck, :], in_=esb[b][:, ck * 128:(ck + 1) * 128], identity=idf[:]),
                             r=[f"esb{b}", "idf"], w=["pT"], inc=(ck == 3))
                    c.op("act", lambda: nc.scalar.copy(out=peT[b][:], in_=pT[:]), r=["pT"], w=[f"peT{b}"])
                    for ck in range(4):
                        c.op("pe", lambda ck=ck: nc.tensor.matmul(pOc[:, hh, :], lhsT=peT[b][:, ck, :], rhs=vct[g][:, ck, :], start=(ck == 0), stop=(ck == 3)),
                             r=[f"peT{b}", ("vc", g)], w=["pOc"], inc=(ck == 3))
                    c.op("dve", lambda: nc.vector.tensor_scalar(out=o_a_sb[:, qt, h * 64:(h + 1) * 64], in0=pOc[:, hh, :], scalar1=sm[b][:, 4:5],
                                                                scalar2=None, op0=ALU.mult),
                         r=["pOc", f"sm{b}"], w=[("o_a", qt)])
                c.op("pool", lambda: nc.gpsimd.tensor_tensor(out=impS[:], in0=imp4[:, 0:128, 0], in1=imp4[:, 0:128, 1], op=ALU.add), r=["imp"], w=["impS"])
                c.op("pool", lambda: nc.gpsimd.tensor_tensor(out=impS[:], in0=impS[:], in1=imp4[:, 0:128, 2], op=ALU.add), r=["imp", "impS"], w=["impS"])
                c.op("pool", lambda: nc.gpsimd.tensor_tensor(out=impS[:], in0=impS[:], in1=imp4[:, 0:128, 3], op=ALU.add), r=["imp", "impS"], w=["impS"])
                c.op("pool", lambda: nc.gpsimd.tensor_tensor(out=impS[:], in0=impS[:], in1=imp4[:, 1:129, 0], op=ALU.add), r=["imp", "impS"], w=["impS"])
                c.op("pool", lambda: nc.gpsimd.tensor_tensor(out=impS[:], in0=impS[:], in1=mbt[qt % 2][:], op=ALU.add), r=[f"mbt{qt % 2}", "impS"], w=["impS"])
                c.op("dve", lambda: nc.vector.max(out=m8[:, 0:8], in_=impS[:]), r=["impS"], w=["m8"])
                c.op("dve", lambda: nc.vector.match_replace(out=imp2[:], in_to_replace=m8[:, 0:8], in_values=impS[:], imm_value=-3e4),
                     r=["impS", "m8"], w=["imp2"])
                c.op("dve", lambda: nc.vector.max(out=m8[:, 8:16], in_=imp2[:]), r=["imp2", "m8"], w=["m8"])
                c.op("dve", lambda: nc.vector.tensor_scalar(out=sel[:], in0=impS[:], scalar1=m8[:, 15:16], scalar2=None, op0=ALU.is_ge),
                     r=["impS", "m8"], w=["sel"])
                c.op("pe", lambda: nc.tensor.transpose(out=pSel[:], in_=sel[:], identity=idf[:]), r=["sel", "idf"], w=["pSel"])
                c.op("act", lambda: nc.scalar.copy(out=selT[:, g, qsl], in_=pSel[:]), r=["pSel"], w=[("selT", g, qt // 4)])


def emit_st_branches(c, g, qa_d, colb_d, KXV_d, VSWV_d, gates_sb, o_a_sb, selT, consts, idf):
    nc = c.nc
    KB = consts["KBIAS"]
    with c.scope() as st:
        qa = c.sb("qa_g", [65, 4, NT], BF16, st)
        c.dma("sp", qa[0:64, :, :], qa_d[:, 4 * g:4 * g + 4, :], w=["qa_g"])
        c.dma("pool", qa[64:65, :, :], colb_d[4 * g:4 * g + 4, :], w=["qa_g"])
        kT = {}
        va = {}
        for bi, (name, widx, voff) in enumerate((("sel", 2, 0), ("win", 3, 128))):
            kT[name] = c.sb(f"kT_{name}", [65, 8192], BF16, st)
            c.dma("sp", kT[name][0:64, :], KXV_d[g * 64:(g + 1) * 64, widx, :], w=[f"kT_{name}"])
            c.op("dve", lambda name=name: nc.vector.memset(kT[name][64:65, :], 1.0), w=[f"kT_{name}"])
            va[name] = c.sb(f"va_{name}", [128, 64, 65], BF16, st)
            c.dma("sp", va[name][:, :, 0:64], VSWV_d[:, voff + g * 64: voff + (g + 1) * 64].rearrange("(kb p) d -> p kb d", p=128), w=[f"va_{name}"])
            c.op("dve", lambda name=name: nc.vector.tensor_copy(out=va[name][:, :, 64], in_=consts["KVALID"][:]), r=["KVALID"], w=[f"va_{name}"])
        esb = [c.sb(f"e{i}", [128, 512], BF16, st) for i in range(3)]
        pmb = [c.sb(f"pm{i}", [128, 512], BF16, st) for i in range(3)]
        msk = [c.sb(f"msk{i}", [128, 512], BF16, st) for i in range(2)]
        oT_sb = c.sb("oT_sb", [65, 512], F32, st)
        rz = c.sb("rz", [128, 4], F32, st)
        oT = [c.ps(f"oT{i}", [65, 512], F32, st) for i in range(4)]
        pS = [c.ps(f"pSs{i}", [128, 512], F32, st) for i in range(2)]
        pM = c.ps("pM", [128, 512], F32, st)
        ptO = c.ps("ptO", [128, 4, 128], F32, st)
        it = 0
        mi = 0
        for j in range(4):
            for br, (name, gcol_off) in enumerate((("sel", 1), ("win", 2))):
                kbs = list(range(0, 16 * j + 16)) if name == "sel" else list(range(16 * j + 8, 16 * j + 16))
                for kb in kbs:
                    di = kb - (16 * j + 12)
                    c0 = 128 * max(di, 0)
                    N = 512 - c0
                    qsl = slice(j * 512 + c0, (j + 1) * 512)
                    ksl = slice(kb * 128, (kb + 1) * 128)
                    if name == "sel":
                        c.op("pe", lambda: nc.tensor.matmul(pM[:, 0:N], lhsT=consts["EALL"][:, kb, :], rhs=selT[:, g, qsl], start=True, stop=True),
                             r=["EALL", ("selT", g, j)], w=["pM"])
                        if di >= 0:
                            mb_ = mi % 2
                            mi += 1
                            c.op("dve", lambda: nc.vector.tensor_tensor(out=msk[mb_][:, 0:N], in0=pM[:, 0:N], in1=consts["DMASK"][:, di, c0:512], op=ALU.mult),
                                 r=["pM", "DMASK"], w=[f"msk{mb_}"])
                            mask_ap, mkey = msk[mb_][:, 0:N], f"msk{mb_}"
                        else:
                            mask_ap, mkey = pM[:, 0:N], "pM"
                    else:
                        wi = kb - (16 * j + 8)
                        mask_ap, mkey = consts["WMASK"][:, wi, c0:512], "WMASK"
                    for hh in range(4):
                        h = 4 * g + hh
                        b = it % 2
                        b3 = it % 3
                        it += 1
                        c.op("pe", lambda: nc.tensor.matmul(pS[b][:, 0:N], lhsT=kT[name][0:65, ksl], rhs=qa[0:65, hh, qsl], start=True, stop=True),
                             r=[f"kT_{name}", "qa_g"], w=[f"pSs{b}"])
                        col = (JB[j] + kb) * 8 + h
                        c.op("act", lambda: nc.scalar.activation(out=esb[b3][:, 0:N], in_=pS[b][:, 0:N], func=AF.Exp, scale=SCALE_NSA,
                                                                 bias=KB[:, col:col + 1]),
                             r=[f"pSs{b}", "KBIAS"], w=[f"e{b3}"])
                        c.op("dve", lambda: nc.vector.tensor_tensor(out=pmb[b3][:, 0:N], in0=esb[b3][:, 0:N], in1=mask_ap, op=ALU.mult),
                             r=[f"e{b3}", mkey], w=[f"pm{b3}"])
                        c.op("pe", lambda: nc.tensor.matmul(oT[hh][:, c0:512], lhsT=va[name][:, kb, 0:65], rhs=pmb[b3][:, 0:N],
                                                            start=(kb == kbs[0]), stop=(kb == kbs[-1])),
                             r=[f"va_{name}", f"pm{b3}"], w=[f"oT{hh}"], inc=(kb == kbs[-1]))
                for hh in range(4):
                    h = 4 * g + hh
                    emit_o_epilogue(c, oT[hh], f"oT{hh}", o_a_sb, "o_a", h, j, gates_sb, 3 * h + gcol_off, False, idf, None, ptO, (oT_sb, rz))


def emit_mla(c, QMT_d, CKVTV_d, KRTV_d, wuk_d, wuv_d, kvnT_d, o_b_sb, consts, idf):
    nc = c.nc
    with c.scope() as st:
        ckvT = c.sb("ckvT", [128, 2, 8192], BF16, st)
        c.dma("sp", ckvT[:], CKVTV_d, w=["ckvT"])
        kvn = c.sb("kvn", [128, 2], F32, st)
        c.dma("sp", kvn[:], kvnT_d, w=["kvn"])
        wuk = c.sb("wuk", [128, 2, 512], BF16, st)
        wuv = c.sb("wuv", [128, 2, 512], BF16, st)
        c.dma("pool", wuk[:], wuk_d.rearrange("(c p) n -> p c n", p=128), w=["wuk"])
        c.dma("pool", wuv[:], wuv_d.rearrange("(c p) n -> p c n", p=128), w=["wuv"])
        for cc in range(2):
            c.op("dve", lambda cc=cc: nc.vector.tensor_scalar(out=wuk[:, cc, :], in0=wuk[:, cc, :], scalar1=kvn[:, cc:cc + 1], scalar2=None, op0=ALU.mult),
                 r=["wuk", "kvn"], w=["wuk"])
            c.op("dve", lambda cc=cc: nc.vector.tensor_scalar(out=wuv[:, cc, :], in0=wuv[:, cc, :], scalar1=kvn[:, cc:cc + 1], scalar2=None, op0=ALU.mult),
                 r=["wuv", "kvn"], w=["wuv"])
        v_all = c.sb("v_all", [128, 64, 4, 65], BF16, st)
        pV = [c.ps(f"pV{i}", [128, 512], F32, st) for i in range(2)]
        for hq in range(4):
            c.op("pool", lambda hq=hq: nc.gpsimd.tensor_copy(out=v_all[:, :, hq, 64], in_=consts["KVALID"][:]), r=["KVALID"], w=["v_all"])

        def compute_v(hp):
            for kb in range(64):
                b = kb % 2
                for cc in range(2):
                    c.op("pe", lambda cc=cc: nc.tensor.matmul(pV[b][:, 0:256], lhsT=ckvT[:, cc, kb * 128:(kb + 1) * 128],
                                                              rhs=wuv[:, cc, hp * 256:(hp + 1) * 256], start=(cc == 0), stop=(cc == 1)),
                         r=["ckvT", "wuv"], w=[f"pV{b}"], inc=(cc == 1))
                src = pV[b][:, 0:256].rearrange("p (h d) -> p h d", d=64)
                if kb % 2 == 0:
                    c.op("act", lambda: nc.scalar.copy(out=v_all[:, kb, :, 0:64], in_=src), r=[f"pV{b}"], w=["v_all"])
                else:
                    c.op("dve", lambda: nc.vector.tensor_copy(out=v_all[:, kb, :, 0:64], in_=src), r=[f"pV{b}"], w=["v_all"])
        kTh = [c.sb(f"kTh{i}", [96, 8192], BF16, st) for i in range(2)]
        qmh = [c.sb(f"qmh{i}", [96, NT], BF16, st) for i in range(2)]
        for i in range(2):
            c.dma("sp", kTh[i][64:96, :], KRTV_d, w=[f"kTh{i}"])
        esb = [c.sb(f"em{i}", [128, 512], BF16, st) for i in range(3)]
        pmb = [c.sb(f"pmm{i}", [128, 512], BF16, st) for i in range(2)]
        oT_sb = c.sb("oT_sbm", [65, 512], F32, st)
        rz = c.sb("rzm", [128, 4], F32, st)
        oT = [c.ps(f"oTm{i}", [65, 512], F32, st) for i in range(2)]
        pS = [c.ps(f"pSm{i}", [128, 512], F32, st) for i in range(2)]
        ptO = c.ps("ptOm", [128, 4, 128], F32, st)
        pK = pV
        it = 0
        oi = 0
        for h in range(8):
            hb = h % 2
            if h % 4 == 0:
                compute_v(h // 4)
            c.dma("sp", qmh[hb][:], QMT_d[:, h, :], w=[f"qmh{hb}"])
            for ch in range(16):
                b = ch % 2
                for cc in range(2):
                    c.op("pe", lambda cc=cc: nc.tensor.matmul(pK[b][0:64, :], lhsT=wuk[:, cc, h * 64:(h + 1) * 64], rhs=ckvT[:, cc, ch * 512:(ch + 1) * 512],
                                                              start=(cc == 0), stop=(cc == 1)),
                         r=["ckvT", "wuk"], w=[f"pV{b}"], inc=(cc == 1))
                if ch % 2 == 0:
                    c.op("act", lambda: nc.scalar.copy(out=kTh[hb][0:64, ch * 512:(ch + 1) * 512], in_=pK[b][0:64, :]), r=[f"pV{b}"], w=[f"kTh{hb}"])
                else:
                    c.op("dve", lambda: nc.vector.tensor_copy(out=kTh[hb][0:64, ch * 512:(ch + 1) * 512], in_=pK[b][0:64, :]), r=[f"pV{b}"], w=[f"kTh{hb}"])
            for j in range(4):
                ob = oi % 2
                oi += 1
                nkb = 16 * j + 16
                for kb in range(nkb):
                    di = kb - (16 * j + 12)
                    c0 = 128 * max(di, 0)
                    N = 512 - c0
                    qsl = slice(j * 512 + c0, (j + 1) * 512)
                    b = it % 2
                    b3 = it % 3
                    it += 1
                    c.op("pe", lambda: nc.tensor.matmul(pS[b][:, 0:N], lhsT=kTh[hb][0:96, kb * 128:(kb + 1) * 128], rhs=qmh[hb][0:96, qsl], start=True, stop=True),
                         r=[f"kTh{hb}", f"qmh{hb}"], w=[f"pSm{b}"])
                    c.op("act", lambda: nc.scalar.activation(out=esb[b3][:, 0:N], in_=pS[b][:, 0:N], func=AF.Exp, scale=SCALE_MLA),
                         r=[f"pSm{b}"], w=[f"em{b3}"])
                    if di >= 0:
                        c.op("dve", lambda: nc.vector.tensor_tensor(out=pmb[b][:, 0:N], in0=esb[b3][:, 0:N], in1=consts["DMASK"][:, di, c0:512], op=ALU.mult),
                             r=[f"em{b3}", "DMASK"], w=[f"pmm{b}"])
                        rhs_ap, rkey = pmb[b][:, 0:N], f"pmm{b}"
                    else:
                        rhs_ap, rkey = esb[b3][:, 0:N], f"em{b3}"
                    c.op("pe", lambda: nc.tensor.matmul(oT[ob][:, c0:512], lhsT=v_all[:, kb, h % 4, 0:65], rhs=rhs_ap, start=(kb == 0), stop=(kb == nkb - 1)),
                         r=["v_all", rkey], w=[f"oTm{ob}"], inc=(kb == nkb - 1))
                emit_o_epilogue(c, oT[ob], f"oTm{ob}", o_b_sb, "o_b", h, j, None, None, True, idf, None, ptO, (oT_sb, rz))


def emit_merge_out(c, o_a_sb, o_b_sb, modT, x1_d, x2_d, mod_d, w_gm_d, wba_d, wbb_d, wout_d, idb):
    nc = c.nc
    with c.scope() as st:
        u2T = c.sb("u2Tb", [128, 8, NT], BF16, st)
        with contextlib.ExitStack() as s0:
            emit_norm_T(c, x1_d, u2T, "u2Tb", modT, 3, 4, idb, s0)
            c.barrier()
        oT = {"a": c.sb("o_aT", [128, 4, NT], BF16, st), "b": c.sb("o_bT", [128, 4, NT], BF16, st)}
        yT = c.sb("yT", [128, 8, NT], BF16, st)
        with contextlib.ExitStack() as s2:
            ob = [c.sb(f"ob{i}", [128, 512], BF16, s2) for i in range(2)]
            ptr = [c.ps(f"mtr{i}", [128, 4, 128], BF16, s2) for i in range(2)]
            it = 0
            for nm, src, okey in (("a", o_a_sb, "o_a"), ("b", o_b_sb, "o_b")):
                for t in range(NTILE):
                    b = it % 2
                    it += 1
                    c.op("act", lambda: nc.scalar.copy(out=ob[b][:], in_=src[:, t, :]), r=[(okey, t)], w=[f"ob{b}"])
                    for cc in range(4):
                        c.op("pe", lambda cc=cc: nc.tensor.transpose(out=ptr[b][:, cc, :], in_=ob[b][:, cc * 128:(cc + 1) * 128], identity=idb[:]),
                             r=[f"ob{b}", "idb"], w=[f"mtr{b}"], inc=(cc == 3))
                    c.op("dve", lambda: nc.vector.tensor_copy(out=oT[nm][:, :, t * 128:(t + 1) * 128], in_=ptr[b][:]), r=[f"mtr{b}"], w=[f"o_{nm}T"])
            c.barrier()
        with contextlib.ExitStack() as s2:
            wb = {"a": c.sb("wba", [128, 4, D], BF16, s2), "b": c.sb("wbb", [128, 4, D], BF16, s2)}
            c.dma("pool", wb["a"][:], wba_d.rearrange("(c p) n -> p c n", p=128), w=["wba"])
            c.dma("pool", wb["b"][:], wbb_d.rearrange("(c p) n -> p c n", p=128), w=["wbb"])
            wgm = [c.sb(f"wgm{i}", [128, 8, 256], BF16, s2) for i in range(2)]
            wv = w_gm_d.rearrange("(k p) n -> p k n", p=128)
            pz = {"a": c.ps("pza", [128, 512], F32, s2), "b": c.ps("pzb", [128, 512], F32, s2)}
            pg = {"a": c.ps("pga", [128, 512], F32, s2), "b": c.ps("pgb", [128, 512], F32, s2)}
            sg = {"a": c.sb("sga", [128, 512], F32, s2), "b": c.sb("sgb", [128, 512], F32, s2)}
            yy = {"a": c.sb("ya", [128, 512], F32, s2), "b": c.sb("yb", [128, 512], F32, s2)}
            for dc in range(8):
                wb_ = dc % 2
                c.dma("pool", wgm[wb_][:, :, 0:128], wv[:, :, dc * 128:(dc + 1) * 128], w=[f"wgm{wb_}"])
                c.dma("pool", wgm[wb_][:, :, 128:256], wv[:, :, 1024 + dc * 128:1024 + (dc + 1) * 128], w=[f"wgm{wb_}"])
                for tg in range(4):
                    tsl = slice(tg * 512, (tg + 1) * 512)
                    for nm, goff in (("a", 0), ("b", 128)):
                        for cc in range(4):
                            c.op("pe", lambda cc=cc, nm=nm: nc.tensor.matmul(pz[nm][:], lhsT=wb[nm][:, cc, dc * 128:(dc + 1) * 128], rhs=oT[nm][:, cc, tsl],
                                                                             start=(cc == 0), stop=(cc == 3)),
                                 r=[f"wb{nm}", f"o_{nm}T"], w=[f"pz{nm}"], inc=(cc == 3))
                        for k in range(8):
                            c.op("pe", lambda k=k, nm=nm, goff=goff: nc.tensor.matmul(pg[nm][:], lhsT=wgm[wb_][:, k, goff:goff + 128],
                                                                                        rhs=u2T[:, k, tsl], start=(k == 0), stop=(k == 7)),
                                 r=[f"wgm{wb_}"] + [("u2Tb", 4 * tg + i) for i in range(4)], w=[f"pg{nm}"], inc=(k == 7))
                        c.op("act", lambda nm=nm: nc.scalar.activation(out=sg[nm][:], in_=pg[nm][:], func=AF.Sigmoid), r=[f"pg{nm}"], w=[f"sg{nm}"])
                        c.op("dve", lambda nm=nm: nc.vector.tensor_tensor(out=yy[nm][:], in0=sg[nm][:], in1=pz[nm][:], op=ALU.mult),
                             r=[f"sg{nm}", f"pz{nm}"], w=[f"y{nm}"])
                    c.op("pool", lambda: nc.gpsimd.tensor_tensor(out=yT[:, dc, tsl], in0=yy["a"][:], in1=yy["b"][:], op=ALU.add),
                         r=["ya", "yb"], w=["yT"])
            c.barrier()
        with contextlib.ExitStack() as s2:
            wo = c.sb("wo", [128, 8, D], BF16, s2)
            c.dma("pool", wo[:], wout_d.rearrange("(k p) n -> p k n", p=128), w=["wo"])
            g2bc = c.sb("g2bc", [128, D], F32, s2)
            c.dma("sp", g2bc[:], mod_d[0:1, 5 * D:6 * D].partition_broadcast(128), r=["mod_d"], w=["g2bc"])
            xh = [c.sb(f"xo{i}", [128, 512], F32, s2) for i in range(2)]
            tmp = [c.sb(f"to{i}", [128, 512], F32, s2) for i in range(2)]
            po = [c.ps(f"poo{i}", [128, 512], F32, s2) for i in range(2)]
            it = 0
            for t in range(NTILE):
                for half in range(2):
                    b = it % 2
                    it += 1
                    hs = slice(half * 512, (half + 1) * 512)
                    c.dma("sp", xh[b][:], x1_d[t * 128:(t + 1) * 128, hs], w=[f"xo{b}"])
                    for k in range(8):
                        c.op("pe", lambda k=k: nc.tensor.matmul(po[b][:], lhsT=yT[:, k, t * 128:(t + 1) * 128], rhs=wo[:, k, hs], start=(k == 0), stop=(k == 7)),
                             r=["yT", "wo"], w=[f"poo{b}"], inc=(k == 7))
                    c.op("dve", lambda: nc.vector.tensor_tensor(out=tmp[b][:], in0=po[b][:], in1=g2bc[:, hs], op=ALU.mult), r=[f"poo{b}", "g2bc"], w=[f"to{b}"])
                    c.op("pool", lambda: nc.gpsimd.tensor_tensor(out=tmp[b][:], in0=tmp[b][:], in1=xh[b][:], op=ALU.add), r=[f"to{b}", f"xo{b}"], w=[f"to{b}"])
                    c.dma("sp", x2_d[t * 128:(t + 1) * 128, hs], tmp[b][:], r=[f"to{b}"], w=["x2_d"])
            c.barrier()


def emit_final_norm(c, x_d, fn_d, out_d):
    nc = c.nc
    with c.scope() as st:
        fnb = c.sb("fnb", [128, D], F32, st)
        c.dma("sp", fnb[:], fn_d[0:1, :].partition_broadcast(128), w=["fnb"])
        xt = [c.sb(f"fx{i}", [128, D], F32, st) for i in range(2)]
        junk = c.sb("fjunk", [128, D], F32, st)
        stt = [c.sb(f"fst{i}", [128, 4], F32, st) for i in range(2)]
        for t in range(NTILE):
            b = t % 2
            c.dma("sp", xt[b][:], x_d[t * 128:(t + 1) * 128, :], w=[f"fx{b}"])
            c.op("act", lambda: nc.scalar.activation(out=junk[:], in_=xt[b][:], func=AF.Square, accum_out=stt[b][:, 0:1]), r=[f"fx{b}"], w=["fjunk", f"fst{b}"])
            c.op("act", lambda: nc.scalar.activation(out=stt[b][:, 1:2], in_=stt[b][:, 0:1], func=AF.Sqrt, scale=1.0 / D, bias=EPS), r=[f"fst{b}"], w=[f"fst{b}"])
            c.op("dve", lambda: nc.vector.reciprocal(out=stt[b][:, 2:3], in_=stt[b][:, 1:2]), r=[f"fst{b}"], w=[f"fst{b}"])
            c.op("dve", lambda: nc.vector.scalar_tensor_tensor(out=xt[b][:], in0=xt[b][:], scalar=stt[b][:, 2:3], in1=fnb[:], op0=ALU.mult, op1=ALU.mult),
                 r=[f"fx{b}", f"fst{b}", "fnb"], w=[f"fx{b}"])
            c.dma("sp", out_d[t * 128:(t + 1) * 128, :], xt[b][:], r=[f"fx{b}"], w=["out_d"])


def emit_A_body(c, idb, idf, a, modT=None):
    if modT is None:
        modT = load_mod(c, a["mod"], idf)
    emit_ffn(c, a["xin"], a["x1"], modT, 0, 1, 2, a["mod"], a["wg1"], a["wu1"], a["wd1"], idb)
    with c.scope() as s0:
        u2T = c.sb("u2T", [128, 8, NT], BF16, s0)
        with c.scope() as s2:
            emit_norm_T(c, a["x1"], u2T, "u2T", modT, 3, 4, idb, s2)
        emit_proj_fm(c, u2T, a["w_inA"], a["QA"], a["KX"])
        emit_proj_tm(c, u2T, a["w_inA"], a["wuq"], a["qnT"], a["cs8"], idb, a["VSW"], a["GATES"], a["CKVT"], a["KRT"], a["QMT"])
    return modT


def emit_B_body(c, idb, idf, a, last):
    modT = load_mod(c, a["mod"], idf)
    consts = {"CEND_d": a["CEND"], "CB_d": a["CB"], "MB_d": a["MB"]}
    with c.scope() as sT:
        for nm, shp in (("QPOS", [128, 16]), ("RV", [128, 16]), ("KVALID", [128, 64]), ("KBIAS", [128, 1280])):
            consts[nm] = c.sb(nm, shp, F32, sT)
            c.dma("sp", consts[nm][:], a[nm], w=[nm])
        gates_sb = c.sb("gates_sb", [128, NTILE, 24], F32, sT)
        c.dma("sp", gates_sb[:], a["GATES"].rearrange("(t p) n -> p t n", p=128), w=["gates_sb"])
        with c.scope() as sB:
            o_a_sb = c.sb("o_a_sb", [128, NTILE, 512], F32, sB)
            o_b_sb = c.sb("o_b_sb", [128, NTILE, 512], F32, sB)
            with c.scope() as sN:
                selT = c.sb("selT", [128, 2, NT], BF16, sN)
                kcT = [c.sb(f"kcT{g}", [64, 512], BF16, sN) for g in range(2)]
                vct = [c.sb(f"vct{g}", [128, 4, 64], BF16, sN) for g in range(2)]
                for g in range(2):
                    emit_compress(c, a["KXV"], g, 0, a["cmpk_w1"], a["cmpk_w2"], a["cmpk_peT"], kcT[g], None, sN)
                    emit_compress(c, a["KXV"], g, 1, a["cmpv_w1"], a["cmpv_w2"], a["cmpv_peT"], None, vct[g], sN)
                emit_cmp_select(c, a["QA"], a["COLB"], gates_sb, o_a_sb, selT, kcT, vct, consts, idf)
                with c.scope() as sC:
                    consts["EALL"] = c.sb("EALL", [128, 64, 128], BF16, sC)
                    c.dma("sp", consts["EALL"][:], a["EALL"], w=["EALL"])
                    consts["DMASK"] = c.sb("DMASK", [128, 4, 512], BF16, sC)
                    c.dma("sp", consts["DMASK"][:], a["DMASK"], w=["DMASK"])
                    consts["WMASK"] = c.sb("WMASK", [128, 8, 512], BF16, sC)
                    c.dma("sp", consts["WMASK"][:], a["WMASK"], w=["WMASK"])
                    for g in range(2):
                        emit_st_branches(c, g, a["QA"], a["COLB"], a["KXV"], a["VSWV"], gates_sb, o_a_sb, selT, consts, idf)
            with c.scope() as sM:
                consts["DMASK"] = c.sb("DMASK2", [128, 4, 512], BF16, sM)
                c.dma("sp", consts["DMASK"][:], a["DMASK"], w=["DMASK"])
                emit_mla(c, a["QMT"], a["CKVTV"], a["KRTV"], a["wuk"], a["wuv"], a["kvnT"], o_b_sb, consts, idf)
            emit_merge_out(c, o_a_sb, o_b_sb, modT, a["x1"], a["x2"], a["mod"], a["w_gm"], a["wba"], a["wbb"], a["wout"], idb)
    emit_ffn(c, a["x2"], a["x3"], modT, 6, 7, 8, a["mod"], a["wg2"], a["wu2"], a["wd2"], idb)
    if last:
        emit_final_norm(c, a["x3"], a["fnorm"], a["out"])


A_IN = (("xin", [NT, D], F32), ("mod", [1, 9216], F32), ("wg1", [D, DFF], F32), ("wu1", [D, DFF], F32), ("wd1", [DFF, D], F32),
        ("w_inA", [D, OFF_GM], F32), ("wuq", [384, 768], F32), ("qnT", [128, 3], F32), ("cs8", [NT, 256], F32))
A_OUT = (("x1", [NT, D], F32), ("QA", [64, 8, NT], BF16), ("KX", [128, 4, NT], BF16), ("VSW", [NT, 256], BF16), ("GATES", [NT, 24], F32),
         ("CKVT", [128, 2, NT], BF16), ("KRT", [32, NT], BF16), ("QMT", [96, 8, NT], BF16))
B_IN = (("x1", [NT, D], F32), ("mod", [1, 9216], F32), ("QA", [64, 8, NT], BF16), ("COLB", [8, NT], F32), ("GATES", [NT, 24], F32),
        ("QMT", [96, 8, NT], BF16), ("KXV", [128, 4, 8192], BF16), ("VSWV", [8192, 256], BF16), ("CKVTV", [128, 2, 8192], BF16),
        ("KRTV", [32, 8192], BF16), ("QPOS", [128, 16], F32), ("RV", [128, 16], F32), ("MB", [16, 128, 128], F32), ("CB", [8, 512], F32),
        ("KVALID", [128, 64], F32), ("CEND", [1, 512], F32), ("KBIAS", [128, 1280], F32), ("DMASK", [128, 4, 512], BF16),
        ("WMASK", [128, 8, 512], BF16), ("EALL", [128, 64, 128], BF16),
        ("cmpk_w1", [2048, 64], F32), ("cmpk_w2", [64, 64], F32), ("cmpk_peT", [64, 32], F32),
        ("cmpv_w1", [2048, 64], F32), ("cmpv_w2", [64, 64], F32), ("cmpv_peT", [64, 32], F32),
        ("wuk", [256, 512], F32), ("wuv", [256, 512], F32), ("kvnT", [128, 2], F32), ("wba", [512, D], F32), ("wbb", [512, D], F32),
        ("w_gm", [D, 2048], F32), ("wout", [D, D], F32), ("wg2", [D, DFF], F32), ("wu2", [D, DFF], F32), ("wd2", [DFF, D], F32))


def build_P():
    nc = bass.Bass("TRN2", target_bir_lowering=False)
    cT_d = nc.dram_tensor("cT2", [2, 128, 8], F32, kind="ExternalInput").ap()
    wada_d = nc.dram_tensor("w_ada", [2, D, 9216], F32, kind="ExternalInput").ap()
    bada_d = nc.dram_tensor("b_ada", [2, 1, 9216], F32, kind="ExternalInput").ap()
    mod_d = nc.dram_tensor("modall", [4, 1, 9216], F32, kind="ExternalOutput").ap()
    with contextlib.ExitStack() as st:
        c = Ctx(nc, st)
        for l in range(2):
            for b in range(2):
                with c.scope() as s1:
                    emit_mod(c, cT_d[b], wada_d[l], bada_d[l], mod_d[2 * l + b], s1)
        c.barrier()
    return nc


def build_U(kind):
    nc = bass.Bass("TRN2", target_bir_lowering=False)
    a = {}
    ident_d = nc.dram_tensor("ident", [128, 128], F32, kind="ExternalInput").ap()
    if kind == "A":
        for nm, shp, dt in A_IN:
            a[nm] = nc.dram_tensor(nm, shp, dt, kind="ExternalInput").ap()
        for nm, shp, dt in A_OUT:
            a[nm] = nc.dram_tensor(nm, shp, dt, kind="ExternalOutput").ap()
    else:
        for nm, shp, dt in B_IN:
            a[nm] = nc.dram_tensor(nm, shp, dt, kind="ExternalInput").ap()
        a["x2"] = nc.dram_tensor("x2s", [NT, D], F32).ap()
        a["x3"] = nc.dram_tensor("x3s", [NT, D], F32).ap()
        if kind == "B":
            a["fnorm"] = nc.dram_tensor("fnorm", [1, D], F32, kind="ExternalInput").ap()
            a["out"] = nc.dram_tensor("out", [NT, D], F32, kind="ExternalOutput").ap()
    with contextlib.ExitStack() as st:
        c = Ctx(nc, st)
        idb, idf = emit_consts(c, ident_d)
        if kind == "A":
            emit_A_body(c, idb, idf, a)
        else:
            emit_B_body(c, idb, idf, a, last=(kind == "B"))
            if kind == "BA":
                a2 = {}
                for nm, shp, dt in A_IN:
                    if nm == "xin":
                        continue
                    a2[nm] = nc.dram_tensor(nm + "_n", shp, dt, kind="ExternalInput").ap()
                a2["xin"] = a["x3"]
                for nm, shp, dt in A_OUT:
                    a2[nm] = nc.dram_tensor(nm + "_n", shp, dt, kind="ExternalOutput").ap()
                c.barrier()
                emit_A_body(c, idb, idf, a2)
        c.barrier()
    return nc


_PROGS = {}


def _prog(name):
    if name not in _PROGS:
        _PROGS[name] = build_P() if name == "P" else build_U(name)
    return _PROGS[name]


def _scatter_seq(core_outs, axis):
    shp = list(core_outs[0].shape)
    shp[axis] = 8192
    full = np.zeros(shp, core_outs[0].dtype)
    for r in range(4):
        sl = [slice(None)] * len(shp)
        sl[axis] = core_positions(r)
        full[tuple(sl)] = core_outs[r]
    return full


def _a_weights(inp, l, cid, mod, sfx=""):
    r = cid % 4
    d = {"mod": mod, "wg1": inp["ffn1_gate"][l], "wu1": inp["ffn1_up"][l], "wd1": inp["ffn1_down"][l],
         "w_inA": np.ascontiguousarray(inp["w_in"][l][:, 0:OFF_GM]), "wuq": inp["mla_w_uq"][l],
         "qnT": np.ascontiguousarray(inp["mla_q_norm"][l].reshape(3, 128).T), "cs8": rope_tables(core_positions(r))}
    return {k + sfx: v for k, v in d.items()}


def _b_inputs(inp, l, cid, oA, full, consts, mod, sfx_in=""):
    b, r = cid // 4, cid % 4
    g = lambda k: oA[k + sfx_in]
    d = dict(consts[r])
    d.update({"x1": g("x1"), "mod": mod, "QA": g("QA"), "GATES": g("GATES"), "QMT": g("QMT"),
              "KXV": to_view(full[b]["KX"], 2, r), "VSWV": to_view(full[b]["VSW"], 0, r),
              "CKVTV": to_view(full[b]["CKVT"], 2, r), "KRTV": to_view(full[b]["KRT"], 1, r),
              "cmpk_w1": inp["cmpk_w1"][l], "cmpk_w2": inp["cmpk_w2"][l], "cmpk_peT": np.ascontiguousarray(inp["cmpk_pe"][l].T),
              "cmpv_w1": inp["cmpv_w1"][l], "cmpv_w2": inp["cmpv_w2"][l], "cmpv_peT": np.ascontiguousarray(inp["cmpv_pe"][l].T),
              "wuk": inp["mla_w_uk"][l], "wuv": inp["mla_w_uv"][l], "kvnT": np.ascontiguousarray(inp["mla_kv_norm"][l].reshape(2, 128).T),
              "wba": inp["w_branch_a"][l], "wbb": inp["w_branch_b"][l], "w_gm": np.ascontiguousarray(inp["w_in"][l][:, OFF_GM:]),
              "wout": inp["w_out"][l], "wg2": inp["ffn2_gate"][l], "wu2": inp["ffn2_up"][l], "wd2": inp["ffn2_down"][l]})
    return d


def _gather_full(res, sfx=""):
    full = {}
    for b in range(2):
        grp = [res[4 * b + r] for r in range(4)]
        full[b] = {"KX": _scatter_seq([g["KX" + sfx] for g in grp], 2), "VSW": _scatter_seq([g["VSW" + sfx] for g in grp], 0),
                   "CKVT": _scatter_seq([g["CKVT" + sfx] for g in grp], 2), "KRT": _scatter_seq([g["KRT" + sfx] for g in grp], 1)}
    return full


def _bf16_consts(r):
    import ml_dtypes
    d = host_consts_B(r)
    d["DMASK"] = np.ascontiguousarray(d["DMASK"].transpose(1, 0, 2)).astype(ml_dtypes.bfloat16)
    d["WMASK"] = np.ascontiguousarray(d["WMASK"].transpose(1, 0, 2)).astype(ml_dtypes.bfloat16)
    d["EALL"] = d["EALL"].astype(ml_dtypes.bfloat16)
    return d


def kernel(**inputs):
    inp = {k: np.asarray(v) for k, v in inputs.items()}
    cores = list(range(8))
    ident = np.eye(128, dtype=np.float32)
    consts = [_bf16_consts(r) for r in range(4)]
    cT2 = np.ascontiguousarray(inp["c"].reshape(2, 8, 128).transpose(0, 2, 1))
    resP = run_bass_kernel_spmd(_prog("P"), [{"cT2": cT2, "w_ada": inp["w_ada"], "b_ada": inp["b_ada"][:, None, :]}], core_ids=[0]).results[0]
    mods = resP["modall"]
    inA = []
    for cid in cores:
        d = _a_weights(inp, 0, cid, mods[cid // 4])
        d["xin"] = np.ascontiguousarray(inp["x"][cid // 4, core_positions(cid % 4)])
        d["ident"] = ident
        inA.append(d)
    resA = run_bass_kernel_spmd(_prog("A"), inA, core_ids=cores).results
    full = _gather_full(resA)
    inBA = []
    for cid in cores:
        d = _b_inputs(inp, 0, cid, resA[cid], full, consts, mods[cid // 4])
        d.update(_a_weights(inp, 1, cid, mods[2 + cid // 4], sfx="_n"))
        inBA.append(d)
    resBA = run_bass_kernel_spmd(_prog("BA"), inBA, core_ids=cores).results
    full = _gather_full(resBA, "_n")
    inB = []
    for cid in cores:
        d = _b_inputs(inp, 1, cid, resBA[cid], full, consts, mods[2 + cid // 4], sfx_in="_n")
        d["fnorm"] = inp["final_norm"][None, :]
        inB.append(d)
    resB = run_bass_kernel_spmd(_prog("B"), inB, core_ids=cores).results
    out = np.zeros((2, 8192, D), np.float32)
    for cid in cores:
        out[cid // 4, core_positions(cid % 4)] = resB[cid]["out"]
    return out


def build_F2(n_layers=2, n_v=4):
    nc = bass.Bass("TRN2", target_bir_lowering=False)
    di = lambda name, shape, dt=F32: nc.dram_tensor(name, list(shape), dt, kind="ExternalInput").ap()
    it = lambda name, shape, dt=F32: nc.dram_tensor(name, list(shape), dt).ap()
    L = 2
    e = {}
    for nm, shp, dt in (("xin", [4, NT, D], F32), ("ident", [128, 128], F32), ("cT", [128, 8], F32),
                        ("w_ada", [L, D, 9216], F32), ("b_ada", [L, 1, 9216], F32),
                        ("wg1", [L, D, DFF], F32), ("wu1", [L, D, DFF], F32), ("wd1", [L, DFF, D], F32),
                        ("wg2", [L, D, DFF], F32), ("wu2", [L, D, DFF], F32), ("wd2", [L, DFF, D], F32),
                        ("w_in", [L, D, 4024], F32), ("wuq", [L, 384, 768], F32), ("qnT", [L, 128, 3], F32),
                        ("cs84", [4, NT, 256], F32), ("COLB", [8, NT], F32), ("QPOS", [128, 16], F32), ("CEND", [1, 512], F32),
                        ("KBIAS", [128, 1280], F32), ("DMASK", [128, 4, 512], BF16), ("WMASK", [128, 8, 512], BF16), ("EALL", [128, 64, 128], BF16),
                        ("MB4", [4, 16, 128, 128], F32), ("CB4", [4, 8, 512], F32), ("KVALID4", [4, 128, 64], F32), ("RV4", [4, 128, 16], F32),
                        ("cmpk_w1", [L, 2048, 64], F32), ("cmpk_w2", [L, 64, 64], F32), ("cmpk_peT", [L, 64, 32], F32),
                        ("cmpv_w1", [L, 2048, 64], F32), ("cmpv_w2", [L, 64, 64], F32), ("cmpv_peT", [L, 64, 32], F32),
                        ("wuk", [L, 256, 512], F32), ("wuv", [L, 256, 512], F32), ("kvnT", [L, 128, 2], F32),
                        ("wba", [L, 512, D], F32), ("wbb", [L, 512, D], F32), ("wout", [L, D, D], F32), ("fnorm", [1, D], F32)):
        e[nm] = di(nm, shp, dt)
    out_d = nc.dram_tensor("out", [4, NT, D], F32, kind="ExternalOutput").ap()
    s = {}
    for nm, shp, dt in (("mod", [1, 9216], F32), ("x1", [4, NT, D], F32), ("x2", [NT, D], F32), ("x3", [4, NT, D], F32),
                        ("QA", [4, 64, 8, NT], BF16), ("GATES", [4, NT, 24], F32), ("QMT", [4, 96, 8, NT], BF16),
                        ("KX", [4, 128, 4, NT], BF16), ("VSW", [4, NT, 256], BF16), ("CKVT", [4, 128, 2, NT], BF16), ("KRT", [4, 32, NT], BF16),
                        ("KXV", [128, 4, 8192], BF16), ("VSWV", [8192, 256], BF16), ("CKVTV", [128, 2, 8192], BF16), ("KRTV", [32, 8192], BF16)):
        s[nm] = it(nm + "_s", shp, dt)
    with contextlib.ExitStack() as st:
        c = Ctx(nc, st)
        idb, idf = emit_consts(c, e["ident"])
        for l in range(n_layers):
            last = (l == n_layers - 1)
            with c.scope() as s1:
                emit_mod(c, e["cT"], e["w_ada"][l], e["b_ada"][l], s["mod"], s1)
            for v in range(n_v):
                a = {"mod": s["mod"], "wg1": e["wg1"][l], "wu1": e["wu1"][l], "wd1": e["wd1"][l], "w_inA": e["w_in"][l][:, 0:OFF_GM],
                     "wuq": e["wuq"][l], "qnT": e["qnT"][l], "cs8": e["cs84"][v],
                     "xin": (e["xin"][v] if l == 0 else s["x3"][v]),
                     "x1": s["x1"][v], "QA": s["QA"][v], "KX": s["KX"][v], "VSW": s["VSW"][v], "GATES": s["GATES"][v],
                     "CKVT": s["CKVT"][v], "KRT": s["KRT"][v], "QMT": s["QMT"][v]}
                emit_A_body(c, idb, idf, a)
                c.barrier()
            for vq in range(n_v):
              with c.scope() as sz:
                zt = c.sb("zt", [128, 4, 512], BF16, sz)
                c.op("dve", lambda: nc.vector.memset(zt[:], 0.0), w=["zt"])
                for Gv in range(16):
                    vs_ = slice(Gv * 512, (Gv + 1) * 512)
                    idx = Gv - 3 + vq
                    if idx < 0:
                        c.dma("sp", s["KXV"][:, :, vs_], zt[:], r=["zt"], w=["KXV"])
                        c.dma("sp", s["CKVTV"][:, :, vs_], zt[:, 0:2, :], r=["zt"], w=["CKVTV"])
                        c.dma("sp", s["KRTV"][:, vs_], zt[0:32, 0, :], r=["zt"], w=["KRTV"])
                        c.dma("sp", s["VSWV"][vs_, :].rearrange("(p a) n -> p a n", p=128), zt[:].rearrange("p a (b n) -> p (a b) n", n=256)[:, 0:4, :],
                              r=["zt"], w=["VSWV"])
                    else:
                        vsrc, j = idx % 4, idx // 4
                        js = slice(j * 512, (j + 1) * 512)
                        c.dma("sp", s["KXV"][:, :, vs_], s["KX"][vsrc][:, :, js], w=["KXV"])
                        c.dma("sp", s["CKVTV"][:, :, vs_], s["CKVT"][vsrc][:, :, js], w=["CKVTV"])
                        c.dma("sp", s["KRTV"][:, vs_], s["KRT"][vsrc][:, js], w=["KRTV"])
                        c.dma("sp", s["VSWV"][vs_, :], s["VSW"][vsrc][js, :], w=["VSWV"])
              if True:
                a = {"mod": s["mod"], "x1": s["x1"][vq], "x2": s["x2"], "x3": s["x3"][vq], "QA": s["QA"][vq], "COLB": e["COLB"],
                     "GATES": s["GATES"][vq], "QMT": s["QMT"][vq], "KXV": s["KXV"], "VSWV": s["VSWV"], "CKVTV": s["CKVTV"], "KRTV": s["KRTV"],
                     "QPOS": e["QPOS"], "RV": e["RV4"][vq], "MB": e["MB4"][vq], "CB": e["CB4"][vq], "KVALID": e["KVALID4"][vq],
                     "CEND": e["CEND"], "KBIAS": e["KBIAS"], "DMASK": e["DMASK"], "WMASK": e["WMASK"], "EALL": e["EALL"],
                     "cmpk_w1": e["cmpk_w1"][l], "cmpk_w2": e["cmpk_w2"][l], "cmpk_peT": e["cmpk_peT"][l],
                     "cmpv_w1": e["cmpv_w1"][l], "cmpv_w2": e["cmpv_w2"][l], "cmpv_peT": e["cmpv_peT"][l],
                     "wuk": e["wuk"][l], "wuv": e["wuv"][l], "kvnT": e["kvnT"][l], "wba": e["wba"][l], "wbb": e["wbb"][l],
                     "w_gm": e["w_in"][l][:, OFF_GM:4024], "wout": e["wout"][l], "wg2": e["wg2"][l], "wu2": e["wu2"][l], "wd2": e["wd2"][l],
                     "fnorm": e["fnorm"], "out": out_d[vq]}
                emit_B_body(c, idb, idf, a, last)
                c.barrier()
    return nc


def inputs_F2(inp, b):
    import ml_dtypes
    cs = [_bf16_consts(r) for r in range(4)]
    d = {"xin": np.stack([inp["x"][b, core_positions(r)] for r in range(4)]),
         "ident": np.eye(128, dtype=np.float32), "cT": np.ascontiguousarray(inp["c"][b].reshape(8, 128).T),
         "w_ada": inp["w_ada"], "b_ada": np.ascontiguousarray(inp["b_ada"][:, None, :]),
         "wg1": inp["ffn1_gate"], "wu1": inp["ffn1_up"], "wd1": inp["ffn1_down"],
         "wg2": inp["ffn2_gate"], "wu2": inp["ffn2_up"], "wd2": inp["ffn2_down"],
         "w_in": inp["w_in"], "wuq": inp["mla_w_uq"],
         "qnT": np.ascontiguousarray(inp["mla_q_norm"].reshape(2, 3, 128).transpose(0, 2, 1)),
         "kvnT": np.ascontiguousarray(inp["mla_kv_norm"].reshape(2, 2, 128).transpose(0, 2, 1)),
         "cs84": np.stack([rope_tables(core_positions(r)) for r in range(4)]),
         "MB4": np.stack([cs[r]["MB"] for r in range(4)]), "CB4": np.stack([cs[r]["CB"] for r in range(4)]),
         "KVALID4": np.stack([cs[r]["KVALID"] for r in range(4)]), "RV4": np.stack([cs[r]["RV"] for r in range(4)]),
         "cmpk_w1": inp["cmpk_w1"], "cmpk_w2": inp["cmpk_w2"], "cmpk_peT": np.ascontiguousarray(inp["cmpk_pe"].transpose(0, 2, 1)),
         "cmpv_w1": inp["cmpv_w1"], "cmpv_w2": inp["cmpv_w2"], "cmpv_peT": np.ascontiguousarray(inp["cmpv_pe"].transpose(0, 2, 1)),
         "wuk": inp["mla_w_uk"], "wuv": inp["mla_w_uv"], "wba": inp["w_branch_a"], "wbb": inp["w_branch_b"], "wout": inp["w_out"],
         "fnorm": np.ascontiguousarray(inp["final_norm"][None, :])}
    for k in ("COLB", "QPOS", "CEND", "KBIAS", "DMASK", "WMASK", "EALL"):
        d[k] = cs[0][k]
    return d


_F2 = []


def kernel(**inputs):
    inp = {k: np.asarray(v) for k, v in inputs.items()}
    if not _F2:
        _F2.append(build_F2())
    res = run_bass_kernel_spmd(_F2[0], [inputs_F2(inp, b) for b in range(2)], core_ids=[0, 1]).results
    out = np.zeros((2, 8192, D), np.float32)
    for b in range(2):
        for r in range(4):
            out[b, core_positions(r)] = res[b]["out"][r]
    return out
```
